# Optimizing a Trainium2 kernel written in Bass

```python
import jax, jax.numpy as jnp
from jax import lax
import numpy as np

D_MODEL = 1024
BATCH = 2
SEQ = 8192
DEPTH = 1

CHUNK = 64
N_PAST_CHUNKS = 8
BAND = (N_PAST_CHUNKS + 1) * CHUNK
ATT_HEADS = 8
ATT_HEAD_DIM = 64
ATT_WIDTH = ATT_HEADS * ATT_HEAD_DIM
MAX_REL = 256
N_REL = 2 * MAX_REL + 1
DN_HEADS = 4
DN_HEAD_DIM = 128
DN_WIDTH = DN_HEADS * DN_HEAD_DIM
CONV_K = 4
D_FF = 4 * D_MODEL
EPS = 1e-6
SPLITS = [ATT_WIDTH, 2 * ATT_WIDTH, 3 * ATT_WIDTH,
          3 * ATT_WIDTH + 3 * DN_WIDTH,
          3 * ATT_WIDTH + 4 * DN_WIDTH,
          3 * ATT_WIDTH + 4 * DN_WIDTH + DN_HEADS]
IN_COLS = 3 * ATT_WIDTH + 4 * DN_WIDTH + 2 * DN_HEADS

kernel_name = "hymba_chunkattn_gdn_hybrid"


def rms_norm(x, g):
    xf = x.astype(jnp.float32)
    y = xf * lax.rsqrt(jnp.mean(xf * xf, axis=-1, keepdims=True) + EPS)
    return (y * g.astype(jnp.float32)).astype(x.dtype)


def l2_norm(x):
    xf = x.astype(jnp.float32)
    return xf * lax.rsqrt(jnp.sum(xf * xf, axis=-1, keepdims=True) + EPS)


def chunked_rel_attention(q, k, v, q_gain, k_gain, rel_bias):
    B, L, H, Dh = q.shape
    nc = L // CHUNK
    q = rms_norm(q, q_gain)
    k = rms_norm(k, k_gain)
    qc = q.reshape(B, nc, CHUNK, H, Dh)
    pad = ((0, 0), (N_PAST_CHUNKS * CHUNK, 0), (0, 0), (0, 0))
    kp = jnp.pad(k, pad).reshape(B, nc + N_PAST_CHUNKS, CHUNK, H, Dh)
    vp = jnp.pad(v, pad).reshape(B, nc + N_PAST_CHUNKS, CHUNK, H, Dh)
    kb = jnp.concatenate([kp[:, j:j + nc] for j in range(N_PAST_CHUNKS + 1)], axis=2)
    vb = jnp.concatenate([vp[:, j:j + nc] for j in range(N_PAST_CHUNKS + 1)], axis=2)
    scores = jnp.einsum('bcqhd,bckhd->bchqk', qc, kb,
                        preferred_element_type=jnp.float32) * (Dh ** -0.5)
    q_pos = N_PAST_CHUNKS * CHUNK + np.arange(CHUNK)
    k_pos = np.arange(BAND)
    rel_idx = np.clip(q_pos[:, None] - k_pos[None, :], -MAX_REL, MAX_REL) + MAX_REL
    bias = rel_bias.astype(jnp.float32)[:, rel_idx]
    scores = scores + bias[None, None]
    key_chunk = jnp.arange(nc)[:, None] + jnp.asarray(k_pos // CHUNK)[None, :] - N_PAST_CHUNKS
    valid = key_chunk >= 0
    scores = jnp.where(valid[None, :, None, None, :], scores, jnp.finfo(jnp.float32).min)
    p = jax.nn.softmax(scores, axis=-1).astype(v.dtype)
    o = jnp.einsum('bchqk,bckhd->bcqhd', p, vb)
    return o.reshape(B, L, H * Dh)


def causal_short_conv(x, w):
    L = x.shape[1]
    xp = jnp.pad(x, ((0, 0), (CONV_K - 1, 0), (0, 0)))
    y = xp[:, 0:L] * w[0]
    for j in range(1, CONV_K):
        y = y + xp[:, j:j + L] * w[j]
    return jax.nn.silu(y)


def gated_delta_rule(q, k, v, g, beta):
    B, L, H, Dk = q.shape
    Dv = v.shape[-1]
    nc = L // CHUNK
    f32 = jnp.float32
    q = l2_norm(q) * (Dk ** -0.5)
    k = l2_norm(k)
    v = v.astype(f32)

    def to_chunks(t):
        return jnp.transpose(t.reshape(B, nc, CHUNK, H, -1), (0, 3, 1, 2, 4))

    q, k, v = to_chunks(q), to_chunks(k), to_chunks(v)
    g = jnp.transpose(g.astype(f32).reshape(B, nc, CHUNK, H), (0, 3, 1, 2))
    beta = jnp.transpose(beta.astype(f32).reshape(B, nc, CHUNK, H), (0, 3, 1, 2))
    decay = jnp.cumsum(g, axis=-1)
    tril = jnp.asarray(np.tril(np.ones((CHUNK, CHUNK), dtype=bool)))
    strict = jnp.asarray(np.tril(np.ones((CHUNK, CHUNK), dtype=bool), -1))
    diff = decay[..., :, None] - decay[..., None, :]
    l_mask = jnp.where(tril, jnp.exp(jnp.where(tril, diff, 0.0)), 0.0)
    k_beta = k * beta[..., None]
    v_beta = v * beta[..., None]
    a_strict = jnp.where(strict, jnp.einsum('bhnid,bhnjd->bhnij', k_beta, k) * l_mask, 0.0)
    t_mat = a_strict + jnp.eye(CHUNK, dtype=f32)
    value = lax.linalg.triangular_solve(t_mat, v_beta, left_side=True, lower=True,
                                        unit_diagonal=True)
    k_cumdecay = lax.linalg.triangular_solve(t_mat, k_beta * jnp.exp(decay)[..., None],
                                             left_side=True, lower=True, unit_diagonal=True)
    attn_intra = jnp.where(tril, jnp.einsum('bhnid,bhnjd->bhnij', q, k) * l_mask, 0.0)
    q_decay = q * jnp.exp(decay)[..., None]
    k_tail = k * jnp.exp(decay[..., -1:] - decay)[..., None]
    chunk_decay = jnp.exp(decay[..., -1])

    def step(S, inp):
        qd, kc, val, att, kt, cd = inp
        v_new = val - jnp.einsum('bhcd,bhde->bhce', kc, S)
        o = jnp.einsum('bhcd,bhde->bhce', qd, S) + jnp.einsum('bhij,bhje->bhie', att, v_new)
        S = S * cd[..., None, None] + jnp.einsum('bhcd,bhce->bhde', kt, v_new)
        return S, o

    xs = tuple(jnp.moveaxis(t, 2, 0) for t in
               (q_decay, k_cumdecay, value, attn_intra, k_tail, chunk_decay))
    S0 = jnp.zeros((B, H, Dk, Dv), f32)
    _, o = lax.scan(step, S0, xs)
    return jnp.transpose(o, (1, 0, 3, 2, 4)).reshape(B, L, H, Dv)


def setup_inputs(seed: int = 0) -> dict:
    key = jax.random.key(seed)
    ks = jax.random.split(key, 16)
    f32 = jnp.float32

    def nrm(k, shape, s):
        return jax.random.normal(k, shape, f32) * s

    x = nrm(ks[0], (BATCH, SEQ, D_MODEL), 1.0)
    mix_norm_gain = 1.0 + nrm(ks[1], (DEPTH, D_MODEL), 0.05)
    w_in = nrm(ks[2], (DEPTH, D_MODEL, IN_COLS), D_MODEL ** -0.5)
    att_q_gain = 1.0 + nrm(ks[3], (DEPTH, ATT_HEAD_DIM), 0.05)
    att_k_gain = 1.0 + nrm(ks[4], (DEPTH, ATT_HEAD_DIM), 0.05)
    rel_bias = nrm(ks[5], (DEPTH, ATT_HEADS, N_REL), 0.2)
    att_out_gain = 1.0 + nrm(ks[6], (DEPTH, ATT_WIDTH), 0.05)
    dn_conv_w = nrm(ks[7], (DEPTH, CONV_K, 3 * DN_WIDTH), CONV_K ** -0.5)
    dn_a_log = jnp.log(jax.random.uniform(ks[8], (DEPTH, DN_HEADS), f32, 1.0, 16.0))
    dt = jnp.exp(jax.random.uniform(ks[9], (DEPTH, DN_HEADS), f32,
                                    float(np.log(1e-3)), float(np.log(1e-1))))
    dn_dt_bias = dt + jnp.log(-jnp.expm1(-dt))
    dn_out_gain = 1.0 + nrm(ks[10], (DEPTH, DN_HEAD_DIM), 0.05)
    w_out = nrm(ks[11], (DEPTH, D_MODEL, D_MODEL), D_MODEL ** -0.5)
    ffn_norm_gain = 1.0 + nrm(ks[12], (DEPTH, D_MODEL), 0.05)
    w_ff1 = nrm(ks[13], (DEPTH, D_MODEL, D_FF), D_MODEL ** -0.5)
    w_ff2 = nrm(ks[14], (DEPTH, D_FF, D_MODEL), D_FF ** -0.5)
    return {"x": x, "mix_norm_gain": mix_norm_gain, "w_in": w_in,
            "att_q_gain": att_q_gain, "att_k_gain": att_k_gain, "rel_bias": rel_bias,
            "att_out_gain": att_out_gain, "dn_conv_w": dn_conv_w, "dn_a_log": dn_a_log,
            "dn_dt_bias": dn_dt_bias, "dn_out_gain": dn_out_gain, "w_out": w_out,
            "ffn_norm_gain": ffn_norm_gain, "w_ff1": w_ff1, "w_ff2": w_ff2}


def reference(x, mix_norm_gain, w_in, att_q_gain, att_k_gain, rel_bias, att_out_gain,
              dn_conv_w, dn_a_log, dn_dt_bias, dn_out_gain, w_out, ffn_norm_gain,
              w_ff1, w_ff2):
    B, L, _ = x.shape
    h = x
    for layer in range(DEPTH):
        u = rms_norm(h, mix_norm_gain[layer])
        proj = u @ w_in[layer]
        aq, ak, av, dqkv, dgate, da, db = jnp.split(proj, SPLITS, axis=-1)
        shp_a = (B, L, ATT_HEADS, ATT_HEAD_DIM)
        o_att = chunked_rel_attention(aq.reshape(shp_a), ak.reshape(shp_a), av.reshape(shp_a),
                                      att_q_gain[layer], att_k_gain[layer], rel_bias[layer])
        o_att = rms_norm(o_att, att_out_gain[layer])
        dqkv = causal_short_conv(dqkv, dn_conv_w[layer])
        dq, dk, dv = jnp.split(dqkv, 3, axis=-1)
        shp_d = (B, L, DN_HEADS, DN_HEAD_DIM)
        log_decay = -jnp.exp(dn_a_log[layer].astype(jnp.float32)) * jax.nn.softplus(
            da.astype(jnp.float32) + dn_dt_bias[layer].astype(jnp.float32))
        beta = jax.nn.sigmoid(db.astype(jnp.float32))
        o_dn = gated_delta_rule(dq.reshape(shp_d), dk.reshape(shp_d), dv.reshape(shp_d),
                                log_decay, beta)
        o_dn = rms_norm(o_dn, dn_out_gain[layer]) * jax.nn.silu(
            dgate.reshape(shp_d).astype(jnp.float32))
        o_dn = o_dn.reshape(B, L, DN_WIDTH).astype(h.dtype)
        mixed = jnp.concatenate([o_att, o_dn], axis=-1) @ w_out[layer]
        h = h + mixed
        z = rms_norm(h, ffn_norm_gain[layer]) @ w_ff1[layer]
        h = h + jnp.square(jax.nn.relu(z)) @ w_ff2[layer]
    return h
```

```python
import os
from contextlib import ExitStack

import numpy as np
import concourse.bass as bass
import concourse.mybir as mybir
from concourse.bass_utils import run_bass_kernel_spmd

F32 = mybir.dt.float32
BF16 = mybir.dt.bfloat16
AF = mybir.ActivationFunctionType
ALU = mybir.AluOpType

D = 1024
SBT = 256
TT = SBT // 128
EPS = 1e-6
BIG = 30000.0
INC = 3592


class Prog:
    ENG = ("pe", "act", "dve", "pool", "sp")
    M = 30000
    K = 8

    def __init__(self):
        self.ops = {e: [] for e in self.ENG}
        self.ndma = {e: 0 for e in self.ENG}
        self.waited = {e: {} for e in self.ENG}
        self.state = {}
        self.efree = {e: 0.0 for e in self.ENG}
        self.tfin = {}
        self.hop_fin = 0.0

    @staticmethod
    def _nm(a):
        if isinstance(a, str):
            return a
        t = getattr(a, "tensor", None)
        return t.name if t is not None else a.name

    @classmethod
    def _key(cls, x):
        if isinstance(x, tuple):
            return (cls._nm(x[0]), x[1])
        return (cls._nm(x), None)

    def _st(self, name):
        st = self.state.get(name)
        if st is None:
            st = {"w": {}, "r": {}}
            self.state[name] = st
        return st

    def op(self, eng, fn, r=(), w=(), dma=False, dur=0.3):
        r2, w2 = [], []
        for x in r:
            if x is None or isinstance(x, (int, float)):
                continue
            k = self._key(x)
            if k[0].startswith("ps"):
                w2.append((k[0], None))
            else:
                r2.append(k)
        for x in w:
            k = self._key(x)
            w2.append((k[0], None) if k[0].startswith("ps") else k)
        r, w = r2, w2
        deps = set()
        idx = len(self.ops[eng])
        if dma:
            n = self.ndma[eng]
            self.ndma[eng] += 1
            tok = ("d", eng, n)
            if n >= self.K:
                deps.add(("d", eng, n - self.K))
        else:
            tok = ("c", eng, idx)
        for (name, sub) in r:
            st = self._st(name)
            subs = [sub, None] if sub is not None else list(st["w"].keys())
            for s in subs:
                t = st["w"].get(s)
                if t is not None:
                    deps.add(t)
        for (name, sub) in w:
            st = self._st(name)
            subs = [sub, None] if sub is not None else list(set(st["w"].keys()) | set(st["r"].keys()))
            for s in subs:
                t = st["w"].get(s)
                if t is not None:
                    deps.add(t)
                for t in st["r"].get(s, {}).values():
                    deps.add(t)
        for (name, sub) in r:
            st = self._st(name)
            rk = (tok[0], tok[1]) if tok[0] == "c" else (tok[0], tok[1], tok[2] % self.K)
            st["r"].setdefault(sub, {})[rk] = tok
        for (name, sub) in w:
            st = self._st(name)
            if sub is None:
                st["w"] = {None: tok}
                st["r"] = {}
            else:
                st["w"][sub] = tok
                st["r"][sub] = {}
        t_start = self.efree[eng]
        for t in deps:
            if t != tok:
                t_start = max(t_start, self.tfin.get(t, 0.0) + (0.1 if (t[0] == "c" and t[1] == eng) else 0.35))
        if dma:
            self.efree[eng] = t_start + 0.1
            t_end = t_start + 2.0 + dur
        else:
            t_end = t_start + dur
            self.efree[eng] = t_end
        self.tfin[tok] = t_end
        self.hop_fin = max(self.hop_fin, t_end)
        waits = {}
        for t in deps:
            if t == tok:
                continue
            if t[0] == "c":
                if t[1] == eng and eng == "pe" and not dma:
                    continue
                wk = ("c", t[1])
            else:
                wk = ("d", t[1], t[2] % self.K)
            if self.waited[eng].get(wk, -1) >= t[2]:
                continue
            if wk not in waits or waits[wk][2] < t[2]:
                waits[wk] = t
        for wk, t in waits.items():
            self.waited[eng][wk] = t[2]
            if t[0] == "c":
                self.ops[t[1]][t[2]]["needed"] = True
        self.ops[eng].append({"fn": fn, "waits": list(waits.values()), "needed": False, "dma": dma, "tok": tok})
        return tok

    def dma_tail_waits(self):
        waits = []
        for e in self.ENG:
            n = self.ndma[e]
            for slot in range(min(n, self.K)):
                last = ((n - 1 - slot) // self.K) * self.K + slot
                waits.append(("d", e, last))
        return waits

    def finish(self):
        self.ops["sp"].append({"fn": None, "waits": self.dma_tail_waits(), "needed": False, "dma": False, "tok": None})

    def emit(self, nc):
        semnames = []
        seen = set()

        def sem(name):
            if name not in seen:
                seen.add(name)
                semnames.append(name)
            return name

        for e in self.ENG:
            c = 0
            for o in self.ops[e]:
                if o["dma"]:
                    n = o["tok"][2]
                    o["inc"] = (sem(f"d{e}{n % self.K}"), 16)
                elif o["needed"]:
                    c += 1
                    k = (c - 1) // self.M
                    o["inc"] = (sem(f"{e}{k}"), 1)
                    o["val"] = c - k * self.M
                else:
                    o["inc"] = None

        def resolve(t):
            if t[0] == "d":
                return (f"d{t[1]}{t[2] % self.K}", 16 * (t[2] // self.K + 1))
            o = self.ops[t[1]][t[2]]
            return (o["inc"][0], o["val"])

        with ExitStack() as stack:
            sems = {name: stack.enter_context(nc.semaphore(name)) for name in semnames}
            with nc.Block() as block:
                def mk(eng):
                    def body(e):
                        for o in self.ops[eng]:
                            for t in o["waits"]:
                                s, v = resolve(t)
                                e.wait_ge(sems[s], v)
                            if o["fn"] is not None:
                                ins = o["fn"](e)
                                if o["inc"] is not None:
                                    ins.then_inc(sems[o["inc"][0]], o["inc"][1])
                    return body
                block.tensor(mk("pe"))
                block.scalar(mk("act"))
                block.vector(mk("dve"))
                block.gpsimd(mk("pool"))
                block.sync(mk("sp"))


def build_program(npre, nown, debug=None):
    NS = npre + nown
    NTOK = NS * SBT
    debug = debug or ()
    nc = bass.Bass("TRN2", target_bir_lowering=False)
    dr = lambda name, shape, kind="ExternalInput": nc.dram_tensor(name, shape, F32, kind=kind).ap()
    xs = dr("xs", [NTOK, D])
    w_in = dr("w_in", [D, INC])
    w_out = dr("w_out", [D, D])
    w_ff1 = dr("w_ff1", [D, 4 * D])
    w_ff2 = dr("w_ff2", [4 * D, D])
    cst = dr("cst", [128, 6, 128])
    bias_d = dr("biasT", [128, 5 * 8 * 128])
    small = dr("small", [128, 96])
    out_d = dr("out", [nown * SBT, D], kind="ExternalOutput")
    dbg_out = {}

    P = Prog()
    root = ExitStack()
    with root:
        sb_bytes = {}
        sfx = [""]

        def SB(stack, name, shape, dt=F32):
            name = name + sfx[0]
            n = 1
            for d_ in shape[1:]:
                n *= d_
            sb_bytes[name] = n * (2 if dt == BF16 else 4)
            if os.environ.get("KDEBUG_SBUF"):
                print("SBUF", name, sb_bytes[name], flush=True)
            return stack.enter_context(nc.sbuf_tensor(name, shape, dt))

        def PS(stack, name, shape, dt=F32):
            return stack.enter_context(nc.psum_tensor(name, shape, dt))

        def fsz(ap):
            n = 1
            for d_ in ap.shape[1:]:
                n *= d_
            return n

        def mm(out, lhsT, rhs, start=True, stop=True, skip=False):
            n = fsz(rhs)
            d_ = 0.12 + n * 0.0003
            if lhsT.dtype == F32:
                d_ *= 2.0
            P.op("pe", lambda e: e.matmul(out, lhsT=lhsT, rhs=rhs, start=start, stop=stop, skip_group_check=skip),
                 r=[lhsT, rhs], w=[out], dur=d_)

        def tr(out, in_, ident):
            P.op("pe", lambda e: e.transpose(out=out, in_=in_, identity=ident), r=[in_, ident], w=[out],
                 dur=0.3 if in_.dtype == F32 else 0.16)

        def act(out, in_, func, bias=0.0, scale=1.0, accum=None):
            d_ = 0.2 + fsz(out) * 0.00095
            if accum is None:
                P.op("act", lambda e: e.activation(out=out, in_=in_, func=func, bias=bias, scale=scale),
                     r=[in_, bias, scale], w=[out], dur=d_)
            else:
                P.op("act", lambda e: e.activation(out=out, in_=in_, func=func, bias=bias, scale=scale, accum_out=accum),
                     r=[in_, bias, scale], w=[out, accum], dur=d_ + 0.1)

        def edur(eng, out):
            return (0.1 + fsz(out) * 0.0023) if eng == "pool" else (0.1 + fsz(out) * 0.0012)

        def tten(eng, out, in0, in1, op):
            P.op(eng, lambda e: e.tensor_tensor(out=out, in0=in0, in1=in1, op=op), r=[in0, in1], w=[out], dur=edur(eng, out))

        def ts(eng, out, in0, s1, s2, op0, op1=None):
            if op1 is None:
                P.op(eng, lambda e: e.tensor_scalar(out=out, in0=in0, scalar1=s1, scalar2=None, op0=op0), r=[in0, s1], w=[out],
                     dur=edur(eng, out))
            else:
                P.op(eng, lambda e: e.tensor_scalar(out=out, in0=in0, scalar1=s1, scalar2=s2, op0=op0, op1=op1),
                     r=[in0, s1, s2], w=[out], dur=edur(eng, out))

        def stt(out, in0, scalar, in1, op0, op1):
            P.op("dve", lambda e: e.scalar_tensor_tensor(out=out, in0=in0, scalar=scalar, in1=in1, op0=op0, op1=op1),
                 r=[in0, scalar, in1], w=[out], dur=0.2 + fsz(out) * 0.0012)

        def cp(eng, out, in_):
            if eng == "act":
                act(out, in_, AF.Copy)
            else:
                P.op(eng, lambda e: e.tensor_copy(out=out, in_=in_), r=[in_], w=[out], dur=edur(eng, out))

        def recip(out, in_):
            P.op("dve", lambda e: e.reciprocal(out=out, in_=in_), r=[in_], w=[out], dur=0.1 + fsz(out) * 0.008)

        def rstd_(out, in_, scale, bias):
            act(out, in_, AF.Ln, bias=bias, scale=scale)
            act(out, out, AF.Exp, scale=-0.5)

        def silu_(out, in_, tmp):
            act(tmp, in_, AF.Exp, scale=-1.0)
            act(tmp, tmp, AF.Ln, bias=1.0)
            act(tmp, tmp, AF.Exp, scale=-1.0)
            tten("dve" if in_.tensor.name.startswith("ps") else "pool", out, in_, tmp, ALU.mult)

        def scale_cast(i, out, in_, sc):
            if i % 2 == 0:
                ts("dve", out, in_, sc, None, ALU.mult)
            else:
                act(out, in_, AF.Copy, scale=sc)

        def memset(eng, ap, val):
            P.op(eng, lambda e: e.memset(ap, val), w=[ap])

        def dma(out, in_, rk=None, wk=None):
            P.op("sp", lambda e: e.dma_start(out=out, in_=in_), r=[rk if rk is not None else in_],
                 w=[wk if wk is not None else out], dma=True, dur=fsz(out) * 0.0004)

        def dump(name, src, shape):
            if name not in debug:
                return
            d = nc.dram_tensor("dbg_" + name, shape, F32, kind="ExternalOutput").ap()
            dbg_out[name] = d
            dma(d, src)

        ident_f = SB(root, "ident_f", [128, 128])
        ones_f = SB(root, "ones_f", [128, 128])
        tri_f = SB(root, "tri_f", [128, 128])
        maskP = SB(root, "maskP", [128, 128])
        maskN = SB(root, "maskN", [128, 128])
        ident_bf = SB(root, "ident_bf", [128, 128], BF16)
        ones_bf = SB(root, "ones_bf", [128, 128], BF16)
        blk_bf = SB(root, "blk_bf", [128, 128], BF16)
        smallt = SB(root, "smallt", [128, 96])
        bar_a = SB(root, "bar_a", [128, 4])
        bar_s = SB(root, "bar_s", [128, 8])
        psb = [PS(root, f"ps{i}", [128, 512]) for i in range(7)]
        psT = PS(root, "psT", [128, 1024], BF16)

        def barrier():
            P.ops["sp"].append({"fn": None, "waits": P.dma_tail_waits(), "needed": False, "dma": False, "tok": None})
            P.op("sp", lambda e: e.dma_start(out=bar_s[:, 4:8], in_=bar_s[:, 0:4]), r=[bar_s], w=[("BAR", "sp"), bar_s], dma=True)
            P.op("pe", lambda e: e.matmul(psb[6][0:1, 0:1], lhsT=ident_bf[:, 0:1], rhs=ident_bf[:, 0:1], start=True, stop=True),
                 r=[ident_bf], w=[("BAR", "pe"), psb[6]])
            P.op("act", lambda e: e.activation(out=bar_a[:, 0:1], in_=ident_f[:, 0:1], func=AF.Copy), r=[ident_f], w=[("BAR", "act")])
            P.op("dve", lambda e: e.memset(bar_a[:, 1:2], 0.0), w=[("BAR", "dve")])
            P.op("pool", lambda e: e.memset(bar_a[:, 2:3], 0.0), w=[("BAR", "pool")])
            P.op("pe", lambda e: e.matmul(psb[6][0:1, 0:1], lhsT=ident_bf[:, 0:1], rhs=ident_bf[:, 0:1], start=True, stop=True),
                 r=["BAR", ident_bf], w=[psb[6]])
            P.op("act", lambda e: e.activation(out=bar_a[:, 0:1], in_=ident_f[:, 0:1], func=AF.Copy), r=["BAR", ident_f])
            P.op("dve", lambda e: e.memset(bar_a[:, 1:2], 0.0), r=["BAR"])
            P.op("pool", lambda e: e.memset(bar_a[:, 2:3], 0.0), r=["BAR"])
            P.op("sp", lambda e: e.dma_start(out=bar_s[:, 4:8], in_=bar_s[:, 0:4]), r=["BAR", bar_s], w=[("BAR2", "sp"), bar_s], dma=True)

        with ExitStack() as s0:
            cst_f = SB(s0, "cst_f", [128, 6, 128])
            dma(cst_f[:], cst)
            dma(smallt[:], small)
            cp("dve", ident_f[:], cst_f[:, 0, :])
            cp("dve", ones_f[:], cst_f[:, 1, :])
            cp("dve", tri_f[:], cst_f[:, 2, :])
            cp("dve", maskP[:], cst_f[:, 4, :])
            cp("dve", maskN[:], cst_f[:, 5, :])
            cp("pool", ident_bf[:], cst_f[:, 0, :])
            cp("pool", ones_bf[:], cst_f[:, 1, :])
            cp("pool", blk_bf[:], cst_f[:, 3, :])
            memset("pool", bar_s[:], 0.0)
            barrier()
        mg = smallt[:, 0:8]
        wog = smallt[:, 8:16]
        fg = smallt[:, 16:24]
        cw = smallt[:, 24:72]
        qg = smallt[:, 72:73]
        kg = smallt[:, 73:74]
        dtb = smallt[:, 74:78]
        alog = smallt[:, 78:82]
        hmask = smallt[:, 82:83]
        nexpA = smallt[:, 84:88]
        qg8 = smallt[:, 88:89]

        halo = SB(root, "halo", [128, 12, 3])
        Sst = SB(root, "Sst", [128, 4, 128])
        Sbf = SB(root, "Sbf", [128, 4, 128], BF16)
        memset("pool", halo[:], 0.0)
        memset("pool", Sst[:], 0.0)
        memset("pool", Sbf[:], 0.0)
        act(nexpA, alog, AF.Exp)
        ts("dve", nexpA, nexpA, -1.0, None, ALU.mult)
        ts("dve", qg8, qg, 0.125, None, ALU.mult)
        phases = [(0, npre - 2, True), (npre - 2, NS, False)] if npre - 2 >= 2 else [(0, NS, False)]
        for ph, (sb_lo, sb_hi, deep) in enumerate(phases):
          with ExitStack() as s1:
            sfx[0] = f"_{ph}"
            NA = 3 if deep else 2
            NBS = 4 if deep else 2
            NS2 = 2 if deep else 1
            if deep:
                Win = SB(s1, "Win", [128, 8, 1544], BF16)

                def Wc(kt, c0, c1, Win=Win):
                    o = 1536 if c0 < 3584 else 3584 - 1536
                    return Win[:, kt, c0 - o:c1 - o]
            else:
                Win = SB(s1, "Win", [128, 8, INC], BF16)
                Wout = SB(s1, "Wout", [128, 8, D], BF16)
                biasT = SB(s1, "biasTb", [128, 5, 8, 128], BF16)

                def Wc(kt, c0, c1, Win=Win):
                    return Win[:, kt, c0:c1]
            with ExitStack() as s1a:
                stg = [SB(s1a, f"stg{i}", [128, INC]) for i in range(2)]
                for kt in range(8):
                    st_ = stg[kt % 2]
                    if deep:
                        dma(st_[:, 0:1536], w_in[kt * 128:(kt + 1) * 128, 1536:3072])
                        dma(st_[:, 1536:1544], w_in[kt * 128:(kt + 1) * 128, 3584:3592])
                        scale_cast(kt, Win[:, kt, :], st_[:, 0:1544], mg[:, kt:kt + 1])
                    else:
                        dma(st_[:], w_in[kt * 128:(kt + 1) * 128, :])
                        scale_cast(kt, Win[:, kt, :], st_[:], mg[:, kt:kt + 1])
                if not deep:
                    for kt in range(8):
                        st_ = stg[kt % 2]
                        dma(st_[:, 0:D], w_out[kt * 128:(kt + 1) * 128, :])
                        scale_cast(kt, Wout[:, kt, :], st_[:, 0:D], wog[:, kt:kt + 1])
                    for r_ in range(5):
                        st_ = stg[r_ % 2]
                        dma(st_[:, 0:1024], bias_d[:, r_ * 1024:(r_ + 1) * 1024])
                        cp(["dve", "act"][r_ % 2], biasT[:, r_, :, :], st_[:, 0:1024].rearrange("p (h q) -> p h q", h=8))
                barrier()

            xt = [SB(s1, f"xt{i}", [128, D]) for i in range(3 if deep else 1)]
            nss = [SB(s1, f"nss{i}", [128, 4]) for i in range(3)]
            ub = [SB(s1, f"ub{i}", [128, D], BF16) for i in range(TT)]
            prw = [SB(s1, f"prw{i}", [128, SBT + 3]) for i in range(3)]
            cv = [SB(s1, f"cv{i}", [128, SBT]) for i in range(3)]
            cs = [SB(s1, f"cs{i}", [128, SBT]) for i in range(3)]
            sq = [SB(s1, f"sq{i}", [128, SBT], BF16) for i in range(4)]
            sd2 = [None, None, None, SB(s1, "sd23", [128, SBT])]
            kraw = [SB(s1, "kraw0", [128, SBT])]
            uT = [SB(s1, f"uT{a}", [128, 8, SBT], BF16) for a in range(NA)]
            dqT = [[SB(s1, f"dqT{a}_{h}", [128, SBT], BF16) for h in range(4)] for a in range(NA)]
            dkT = [[SB(s1, f"dkT{a}_{h}", [128, SBT], BF16) for h in range(4)] for a in range(NA)]
            dvT = [SB(s1, f"dvT{h}", [128, SBT], BF16) for h in range(4)]
            Ktm = [SB(s1, f"Ktm{a}", [128, 4, TT, 128], BF16) for a in range(NA)]
            Vtm = [SB(s1, f"Vtm{a}", [128, 4, TT, 128], BF16) for a in range(NA)]
            ab = [SB(s1, f"ab{i}", [128, 8]) for i in range(TT)]
            gsm = [[SB(s1, f"gsm{a}_{i}", [128, 24]) for i in range(TT)] for a in range(NA)]
            Gb_ = [SB(s1, f"Gb{k}", [128, 4, 128]) for k in range(NS2)]
            Lm_ = [SB(s1, f"Lm{k}", [128, 4, 128]) for k in range(NS2)]
            LTm_ = Gb_
            At_ = [[SB(s1, f"At{k}_{i}", [128, 4, 128], BF16) for i in range(2)] for k in range(NS2)]
            Bt_ = [[SB(s1, f"Bt{k}_{i}", [128, 4, 128], BF16) for i in range(2)] for k in range(NS2)]
            XB = [SB(s1, f"XB{i}", [128, 4, 128], BF16) for i in range(NBS)]
            TTb = [SB(s1, f"TTb{i}", [128, 4, 128], BF16) for i in range(NBS)]
            PTd = [SB(s1, f"PTd{i}", [128, 4, 128], BF16) for i in range(NBS)]
            Vb = [SB(s1, f"Vb{i}", [128, 4, 128]) for i in range(NBS)]
            Kd = [SB(s1, f"Kd{i}", [128, 4, 128], BF16) for i in range(NBS)]
            dsm = [SB(s1, f"dsm{i}", [128, 32]) for i in range(NBS)]
            Rr = SB(s1, "Rr", [128, 4, 128], BF16)
            Vn0 = SB(s1, "Vn0", [128, 4, 128], BF16)
            rres = SB(s1, "rres", [128, 4, 128], BF16)
            Vn = SB(s1, "Vn", [128, 4, 128], BF16)
            O1 = SB(s1, "O1", [128, 4, 128])
            Odn = SB(s1, "Odn", [128, 4, 128])
            if not deep:
                NR = 4
                akT = [SB(s1, f"akT{i}", [128, 4, SBT], BF16) for i in range(NR)]
                vaug = [SB(s1, f"vaug{i}", [128, TT, 8, 65], BF16) for i in range(NR)]
                aqTz = SB(s1, "aqTz", [128, 4, 2, 128], BF16)
                sg = SB(s1, "sg", [128, 512])
                PTa = [SB(s1, f"PTa{i}", [128, 512], BF16) for i in range(2)]
                oatt = SB(s1, "oatt", [128, 512])
                rec = SB(s1, "rec", [128, 16])
                cat = SB(s1, "cat", [128, D], BF16)
                catT = SB(s1, "catT", [128, D], BF16)
                xres_ = SB(s1, "xres", [128, D])
                qraw_ = SB(s1, "qraw4", [128, 512])
                qsd_ = SB(s1, "qsd4", [128, 512])

            if not deep:
                memset("pool", aqTz[:], 0.0)
                for i in range(NR):
                    memset("pool", vaug[i][:], 1.0)

            pcnt = [0]

            def next_pp():
                pcnt[0] += 1
                return psb[pcnt[0] % 2]
            dcnt = [0]

            pd_sets = [[2, 3], [5, 6]] if deep else [[2, 3]]
            dcnt2 = [0, 0]

            def next_pd_of(k2):
                def f():
                    dcnt2[k2] += 1
                    return psb[pd_sets[k2][dcnt2[k2] % 2]]
                return f
            pS3 = psb[4]

            def S1(sb):
                A = sb % NA
                att = sb >= npre - 2
                for tt in range(TT):
                    x_ = xt[(sb * TT + tt) % len(xt)]
                    ns_ = nss[tt]
                    dma(x_[:], xs[sb * SBT + tt * 128: sb * SBT + (tt + 1) * 128, :])
                    act(ub[tt][:], x_[:], AF.Square, accum=ns_[:, 0:1])
                    rstd_(ns_[:, 2:3], ns_[:, 0:1], 1.0 / D, EPS)
                    scale_cast(tt, ub[tt][:], x_[:], ns_[:, 2:3])
                    yield
                for kt in range(8):
                    for tt in range(TT):
                        tr(psT[:, tt * 128:(tt + 1) * 128], ub[tt][:, kt * 128:(kt + 1) * 128], ident_bf[:])
                    cp("act" if kt % 2 else "dve", uT[A][:, kt, :], psT[:, 0:SBT])
                    if kt % 2:
                        yield
                groups = [S1grp(sb, 1), S1grp(sb, 2), S1ab(sb)]
                if sb >= npre - 1:
                    groups.insert(0, S1grp(sb, 0))
                yield ("spawn", groups)

            def S1grp(sb, grp):
                A = sb % NA
                pw, cv_, cs_, sq_, sd_ = prw[grp], cv[grp], cs[grp], sq[grp], cv[grp]
                for h in range(4):
                    ft = 4 * grp + h
                    pp = next_pp()
                    c0 = 1536 + 128 * ft
                    for kt in range(8):
                        mm(pp[:, 0:SBT], Wc(kt, c0, c0 + 128), uT[A][:, kt, :], start=(kt == 0), stop=(kt == 7))
                    cp("act", pw[:, 3:3 + SBT], pp[:, 0:SBT])
                    P.op("pool", lambda e, pw=pw, ft=ft: e.tensor_copy(out=pw[:, 0:3], in_=halo[:, ft, :]), r=[(halo, ft)], w=[pw])
                    yield
                    ts("dve", cv_[:], pw[:, 0:SBT], cw[:, ft * 4:ft * 4 + 1], None, ALU.mult)
                    for j in range(1, 4):
                        stt(cv_[:], pw[:, j:j + SBT], cw[:, ft * 4 + j:ft * 4 + j + 1], cv_[:], ALU.mult, ALU.add)
                    P.op("pool", lambda e, pw=pw, ft=ft: e.tensor_copy(out=halo[:, ft, :], in_=pw[:, SBT:SBT + 3]), r=[pw], w=[(halo, ft)])
                    silu_(cs_[:], cv_[:], cs_[:])
                    yield
                    if grp < 2:
                        tten("pool", sq_[:], cs_[:], cs_[:], ALU.mult)
                        pn_ = next_pp()
                        mm(pn_[:, 0:SBT], ones_bf[:], sq_[:])
                        sc = 128.0 if grp == 0 else 1.0
                        rstd_(sd_[:], pn_[:, 0:SBT], sc, EPS * sc)
                        tten("dve", (dqT if grp == 0 else dkT)[A][h][:], cs_[:], sd_[:], ALU.mult)
                    else:
                        cp("pool", dvT[h][:], cs_[:])
                    yield
                if grp > 0:
                    srcT, dst = (dkT[A], Ktm[A]) if grp == 1 else (dvT, Vtm[A])
                    for h in range(4):
                        for tt in range(TT):
                            tr(psT[:, (h * TT + tt) * 128:(h * TT + tt + 1) * 128], srcT[h][:, tt * 128:(tt + 1) * 128], ident_bf[:])
                    cp("act", dst[:].rearrange("p h t d -> p (h t d)"), psT[:, 0:4 * TT * 128])
                    yield

            def S1ab(sb):
                A = sb % NA
                att = sb >= npre - 2
                for tt in range(TT):
                    pn_ = next_pp()
                    for kt in range(8):
                        mm(pn_[:, 0:8], uT[A][:, kt, tt * 128:(tt + 1) * 128], Wc(kt, 3584, 3592), start=(kt == 0), stop=(kt == 7))
                    cp("dve", ab[tt][:], pn_[:, 0:8])
                    gs = gsm[A][tt]
                    tten("dve", gs[:, 0:4], ab[tt][:, 0:4], dtb, ALU.add)
                    act(gs[:, 4:8], gs[:, 0:4], AF.Exp)
                    act(gs[:, 8:12], gs[:, 4:8], AF.Ln, bias=1.0)
                    tten("dve", gs[:, 12:16], gs[:, 8:12], nexpA, ALU.mult)
                    act(gs[:, 16:20], ab[tt][:, 4:8], AF.Exp, scale=-1.0)
                    act(gs[:, 16:20], gs[:, 16:20], AF.Ln, bias=1.0)
                    act(gs[:, 16:20], gs[:, 16:20], AF.Exp, scale=-1.0)
                    ts("dve", gs[:, 20:24], gs[:, 16:20], -1.0, None, ALU.mult)
                    yield
                if att:
                    slot = sb % NR
                    kr, sq_, sd_ = kraw[0], sq[3], sd2[3]
                    for p_ in range(4):
                        pp = next_pp()
                        c0 = 512 + 128 * p_
                        for kt in range(8):
                            mm(pp[:, 0:SBT], Wc(kt, c0, c0 + 128), uT[A][:, kt, :], start=(kt == 0), stop=(kt == 7))
                        cp("act", kr[:], pp[:, 0:SBT])
                        tten("pool", sq_[:], kr[:], kr[:], ALU.mult)
                        yield
                        pn_ = next_pp()
                        mm(pn_[:, 0:SBT], blk_bf[:], sq_[:])
                        rstd_(sd_[:], pn_[:, 0:SBT], 1.0 / 64, EPS)
                        stt(akT[slot][:, p_, :], kr[:], kg, sd_[:], ALU.mult, ALU.mult)
                        yield
                    for tt in range(TT):
                        pp = next_pp()
                        for kt in range(8):
                            mm(pp[:, 0:512], uT[A][:, kt, tt * 128:(tt + 1) * 128], Win[:, kt, 1024:1536], start=(kt == 0), stop=(kt == 7))
                        cp("act", vaug[slot][:, tt, :, 0:64], pp[:, 0:512].rearrange("p (h d) -> p h d", h=8))
                        yield

            def S2(blk):
                sb, tt = divmod(blk, TT)
                A = sb % NA
                Bs = blk % NBS
                own = sb >= npre
                tsl = slice(tt * 128, (tt + 1) * 128)
                k2 = blk % NS2
                next_pd = next_pd_of(k2)
                Gb, Lm, LTm, At, Bt = Gb_[k2], Lm_[k2], LTm_[k2], At_[k2], Bt_[k2]
                gs = gsm[A][tt]
                g_ = gs[:, 12:16]
                nbeta = gs[:, 20:24]
                ds_ = dsm[Bs]
                dcol, dlast, ed, nbed, dd, edl, cd = (ds_[:, 4 * i:4 * i + 4] for i in range(7))
                pc = next_pd()
                mm(pc[:, 0:4], tri_f[:], g_)
                mm(pc[:, 4:8], ones_f[:], g_)
                cp("dve", ds_[:, 0:8], pc[:, 0:8])
                tten("dve", Gb[:], ones_f[:].unsqueeze(1).to_broadcast([128, 4, 128]),
                     g_.unsqueeze(2).to_broadcast([128, 4, 128]), ALU.mult)
                yield
                act(ed, dcol, AF.Exp)
                tten("dve", nbed, ed, nbeta, ALU.mult)
                tten("dve", dd, dlast, dcol, ALU.subtract)
                act(edl, dd, AF.Exp)
                act(cd, dlast, AF.Exp)
                pD = next_pd()
                for h in range(4):
                    mm(pD[:, h * 128:(h + 1) * 128], Gb[:, h, :], tri_f[:])
                yield
                for h in range(4):
                    stt(Lm[:, h, :], pD[:, h * 128:(h + 1) * 128], dcol[:, h:h + 1], maskP[:], ALU.subtract, ALU.max)
                if own:
                    for h in range(4):
                        stt(LTm[:, h, :], pD[:, h * 128:(h + 1) * 128], dcol[:, h:h + 1], maskN[:], ALU.subtract, ALU.min)
                act(Lm[:], Lm[:], AF.Exp, scale=-1.0)
                if own:
                    act(LTm[:], LTm[:], AF.Exp)
                pK = next_pd()
                for h in range(4):
                    mm(pK[:, h * 128:(h + 1) * 128], dkT[A][h][:, tsl], dkT[A][h][:, tsl])
                yield
                A0, B0, X = At[0], Bt[0], XB[Bs]
                for h in range(4):
                    stt(A0[:, h, :], pK[:, h * 128:(h + 1) * 128], nbeta[:, h:h + 1], Lm[:, h, :], ALU.mult, ALU.mult)
                if own:
                    pQ = next_pd()
                    for h in range(4):
                        mm(pQ[:, h * 128:(h + 1) * 128], dkT[A][h][:, tsl], dqT[A][h][:, tsl])
                    tten("dve", PTd[Bs][:].rearrange("p h c -> p (h c)"), pQ[:, 0:512], LTm[:].rearrange("p h c -> p (h c)"), ALU.mult)
                yield
                pBt = next_pd()[:].bitcast(BF16)
                for h in range(4):
                    tr(pBt[:, h * 128:(h + 1) * 128], A0[:, h, :], ident_bf[:])
                cp("act", B0[:].rearrange("p h c -> p (h c)"), pBt[:, 0:512])
                tten("pool", X[:], B0[:], ident_f[:].unsqueeze(1).to_broadcast([128, 4, 128]), ALU.add)
                tten("pool", TTb[Bs][:], ident_f[:].unsqueeze(1).to_broadcast([128, 4, 128]), B0[:], ALU.subtract)
                yield
                Ap, Bp = A0, B0
                for lvl in range(1, 6):
                    An, Bn = At[lvl % 2], Bt[lvl % 2]
                    pA = next_pd()
                    for h in range(4):
                        mm(pA[:, h * 128:(h + 1) * 128], Bp[:, h, :], Ap[:, h, :])
                    cp("act", An[:].rearrange("p h c -> p (h c)"), pA[:, 0:512])
                    if lvl < 5:
                        pB = next_pd()
                        for h in range(4):
                            mm(pB[:, h * 128:(h + 1) * 128], Ap[:, h, :], Bp[:, h, :])
                        cp("dve", Bn[:].rearrange("p h c -> p (h c)"), pB[:, 0:512])
                    yield
                    pX = next_pd()
                    for h in range(4):
                        mm(pX[:, h * 128:(h + 1) * 128], An[:, h, :], X[:, h, :])
                    tten("dve", X[:].rearrange("p h c -> p (h c)"), pX[:, 0:512], X[:].rearrange("p h c -> p (h c)"), ALU.add)
                    yield
                    Ap, Bp = An, Bn
                tten("pool", Vb[Bs][:], Vtm[A][:, :, tt, :], gs[:, 16:20].unsqueeze(2).to_broadcast([128, 4, 128]), ALU.mult)
                tten("pool", Kd[Bs][:], Ktm[A][:, :, tt, :], edl.unsqueeze(2).to_broadcast([128, 4, 128]), ALU.mult)
                yield

            def S3a(blk):
                sb, tt = divmod(blk, TT)
                A = sb % NA
                Bs = blk % NBS
                own = sb >= npre
                tsl = slice(tt * 128, (tt + 1) * 128)
                ds_ = dsm[Bs]
                dcol, dlast, ed, nbed, dd, edl, cd = (ds_[:, 4 * i:4 * i + 4] for i in range(7))
                X = XB[Bs]
                for h in range(4):
                    mm(pS3[:, h * 128:(h + 1) * 128], dkT[A][h][:, tsl], Sbf[:, h, :])
                for h in range(4):
                    stt(Rr[:, h, :], pS3[:, h * 128:(h + 1) * 128], nbed[:, h:h + 1], Vb[Bs][:, h, :], ALU.mult, ALU.add)
                yield
                for h in range(4):
                    mm(pS3[:, h * 128:(h + 1) * 128], X[:, h, :], Rr[:, h, :])
                cp("act", Vn0[:].rearrange("p h c -> p (h c)"), pS3[:, 0:512])
                yield
                for h in range(4):
                    mm(pS3[:, h * 128:(h + 1) * 128], TTb[Bs][:, h, :], Vn0[:, h, :])
                stt(rres[:].rearrange("p h c -> p (h c)"), pS3[:, 0:512], -1.0, Rr[:].rearrange("p h c -> p (h c)"), ALU.mult, ALU.add)
                yield
                for h in range(4):
                    mm(pS3[:, h * 128:(h + 1) * 128], X[:, h, :], rres[:, h, :])
                tten("dve", Vn[:].rearrange("p h c -> p (h c)"), pS3[:, 0:512], Vn0[:].rearrange("p h c -> p (h c)"), ALU.add)
                yield
                if own:
                    for h in range(4):
                        mm(pS3[:, h * 128:(h + 1) * 128], dqT[A][h][:, tsl], Sbf[:, h, :])
                    for h in range(4):
                        ts("dve", O1[:, h, :], pS3[:, h * 128:(h + 1) * 128], ed[:, h:h + 1], None, ALU.mult)
                    yield
                    for h in range(4):
                        mm(pS3[:, h * 128:(h + 1) * 128], PTd[Bs][:, h, :], Vn[:, h, :])
                    tten("dve", Odn[:].rearrange("p h c -> p (h c)"), pS3[:, 0:512], O1[:].rearrange("p h c -> p (h c)"), ALU.add)
                    yield
                for h in range(4):
                    mm(pS3[:, h * 128:(h + 1) * 128], Kd[Bs][:, h, :], Vn[:, h, :])
                for h in range(4):
                    stt(Sst[:, h, :], Sst[:, h, :], cd[:, h:h + 1], pS3[:, h * 128:(h + 1) * 128], ALU.mult, ALU.add)
                cp("act", Sbf[:], Sst[:])
                yield
                if not own:
                    return
                ob = sb - npre
                row0 = ob * SBT + tt * 128
                T0 = sb * SBT + tt * 128
                if "odn" in debug:
                    dmp = dbg_out.get("odn")
                    if dmp is None:
                        dmp = nc.dram_tensor("dbg_odn", [nown * SBT, 512], F32, kind="ExternalOutput").ap()
                        dbg_out["odn"] = dmp
                    dma(dmp[row0:row0 + 128, :], Odn[:].rearrange("p h c -> p (h c)"), wk=("dbgodn", row0))
                for h in range(4):
                    act(cat[:, 512 + h * 128:512 + (h + 1) * 128], Odn[:, h, :], AF.Square, accum=rec[:, h:h + 1])
                rstd_(rec[:, 8:12], rec[:, 0:4], 1.0 / 128, EPS)
                pg = psb[4]
                for kt in range(8):
                    mm(pg[:, 0:512], uT[A][:, kt, tsl], Win[:, kt, 3072:3584], start=(kt == 0), stop=(kt == 7))
                silu_(sg[:], pg[:, 0:512], sg[:])
                for h in range(4):
                    stt(cat[:, 512 + h * 128:512 + (h + 1) * 128], Odn[:, h, :], rec[:, 8 + h:9 + h], sg[:, h * 128:(h + 1) * 128],
                        ALU.mult, ALU.mult)
                yield

            def S4att(blk):
                sb, tt = divmod(blk, TT)
                A = sb % NA
                tsl = slice(tt * 128, (tt + 1) * 128)
                ob = sb - npre
                row0 = ob * SBT + tt * 128
                T0 = sb * SBT + tt * 128
                pq, pq2 = psb[5], psb[6]
                qraw4 = qraw_[:]
                qsd4 = qsd_[:]
                qsq4 = cat[:, 0:512]
                for p_ in range(4):
                    c0 = 128 * p_
                    for kt in range(8):
                        mm(pq[:, p_ * 128:(p_ + 1) * 128], Win[:, kt, c0:c0 + 128], uT[A][:, kt, tsl], start=(kt == 0), stop=(kt == 7))
                cp("act", qraw4, pq[:, 0:512])
                tten("pool", qsq4, qraw4, qraw4, ALU.mult)
                yield
                for p_ in range(4):
                    mm(pq2[:, p_ * 128:(p_ + 1) * 128], blk_bf[:], qsq4[:, p_ * 128:(p_ + 1) * 128])
                rstd_(qsd4, pq2[:, 0:512], 1.0 / 64, EPS)
                for p_ in range(4):
                    for e_ in range(2):
                        stt(aqTz[64 * e_:64 * e_ + 64, p_, e_, :], qraw4[64 * e_:64 * e_ + 64, p_ * 128:(p_ + 1) * 128],
                            qg8[64 * e_:64 * e_ + 64, :], qsd4[64 * e_:64 * e_ + 64, p_ * 128:(p_ + 1) * 128], ALU.mult, ALU.mult)
                yield
                for hg in range(2):
                    pOa = psb[6]
                    for r_ in range(5):
                        st_tok = T0 - 512 + 128 * r_
                        sbk = st_tok // SBT
                        ttk = (st_tok % SBT) // 128
                        slk = sbk % NR
                        pSa = psb[5]
                        for hh in range(4):
                            h = 4 * hg + hh
                            p_, e_ = h // 2, h % 2
                            mm(pSa[:, hh * 128:(hh + 1) * 128], ident_bf[:], biasT[:, r_, h, :], start=True, stop=False)
                            mm(pSa[:, hh * 128:(hh + 1) * 128], akT[slk][:, p_, ttk * 128:(ttk + 1) * 128],
                               aqTz[:, p_, e_, :], start=False, stop=True)
                        pt_ = PTa[r_ % 2]
                        act(pt_[:], pSa[:, 0:512], AF.Exp, bias=(hmask if sbk < npre else 0.0))
                        for hh in range(4):
                            h = 4 * hg + hh
                            mm(pOa[:, hh * 65:hh * 65 + 65], pt_[:, hh * 128:(hh + 1) * 128], vaug[slk][:, ttk, h, :],
                               start=(r_ == 0 and hh == 0), stop=(r_ == 4), skip=True)
                        yield
                    ov = pOa[:, 0:260].rearrange("p (h c) -> p h c", c=65)
                    recip(rec[:, 12:16], ov[:, :, 64])
                    tten("dve", oatt[:, hg * 256:(hg + 1) * 256].rearrange("p (h d) -> p h d", d=64), ov[:, :, 0:64],
                         rec[:, 12:16].unsqueeze(2).to_broadcast([128, 4, 64]), ALU.mult)
                    yield
                if "oatt" in debug:
                    dmp = dbg_out.get("oatt")
                    if dmp is None:
                        dmp = nc.dram_tensor("dbg_oatt", [nown * SBT, 512], F32, kind="ExternalOutput").ap()
                        dbg_out["oatt"] = dmp
                    dma(dmp[row0:row0 + 128, :], oatt[:], wk=("dbgoatt", row0))

            def S4fin(blk):
                sb, tt = divmod(blk, TT)
                ob = sb - npre
                row0 = ob * SBT + tt * 128
                T0 = sb * SBT + tt * 128
                ns_ = nss[2]
                act(cat[:, 0:512], oatt[:], AF.Square, accum=ns_[:, 0:1])
                rstd_(ns_[:, 2:3], ns_[:, 0:1], 1.0 / 512, EPS)
                ts("dve", cat[:, 0:512], oatt[:], ns_[:, 2:3], None, ALU.mult)
                for kt in range(8):
                    tr(psT[:, kt * 128:(kt + 1) * 128], cat[:, kt * 128:(kt + 1) * 128], ident_bf[:])
                cp("act", catT[:], psT[:, 0:1024])
                dma(xres_[:], xs[T0:T0 + 128, :])
                yield
                for half in range(2):
                    pp = psb[5 + half]
                    for kt in range(8):
                        mm(pp[:, 0:512], catT[:, kt * 128:(kt + 1) * 128], Wout[:, kt, half * 512:(half + 1) * 512],
                           start=(kt == 0), stop=(kt == 7))
                    tten("dve", xres_[:, half * 512:(half + 1) * 512], pp[:, 0:512], xres_[:, half * 512:(half + 1) * 512], ALU.add)
                dma(out_d[row0:row0 + 128, :], xres_[:], wk=("dout", row0))
                yield

            def S34(blk):
                sb = blk // TT
                if sb < npre:
                    yield from S3a(blk)
                    return
                if os.environ.get("KSEQ"):
                    yield from S3a(blk)
                    yield from S4att(blk)
                    yield from S4fin(blk)
                    return
                g_att = S4att(blk)
                g_s3 = S3a(blk)
                km = os.environ.get("KMODE", "2")
                if km == "1":
                    next(g_att)
                    yield
                    next(g_att)
                    yield
                elif km == "2":
                    for _ in range(2):
                        next(g_att)
                        try:
                            next(g_s3)
                        except StopIteration:
                            pass
                        yield
                    for _ in g_s3:
                        yield
                subs = [[0.0, 0, g_s3], [0.0, 1, g_att]]
                while subs:
                    subs.sort(key=lambda t_: (t_[0], t_[1]))
                    ent = subs[0]
                    h0 = P.hop_fin
                    P.hop_fin = 0.0
                    try:
                        next(ent[2])
                    except StopIteration:
                        subs.pop(0)
                        P.hop_fin = max(h0, P.hop_fin)
                        continue
                    if P.hop_fin > 0.0:
                        ent[0] = P.hop_fin
                    P.hop_fin = max(h0, P.hop_fin)
                    yield
                yield from S4fin(blk)

            def run_round(gens):
                now = min(P.efree.values())
                live = [[now, i, g] for i, (g, w) in enumerate(gens)]
                cnt = len(live)
                while live:
                    live.sort(key=lambda t: (t[0], t[1]))
                    ent = live[0]
                    P.hop_fin = 0.0
                    try:
                        y = next(ent[2])
                    except StopIteration:
                        live.pop(0)
                        continue
                    if P.hop_fin > 0.0:
                        ent[0] = P.hop_fin
                    if isinstance(y, tuple) and y[0] == "spawn":
                        for g2 in y[1]:
                            live.append([ent[0], cnt, g2])
                            cnt += 1

            def S34pair(R):
                for t_ in range(TT):
                    yield from S34(R * TT + t_)

            run_round([(S1(sb_lo), 1)])
            if deep:
                for R in range(sb_lo, sb_hi + 1):
                    gens = []
                    if R - 1 >= sb_lo:
                        gens.append((S34pair(R - 1), 1))
                    if R < sb_hi:
                        gens += [(S2(R * TT + t_), 1) for t_ in range(TT)]
                    if R + 1 < sb_hi:
                        gens.append((S1(R + 1), 1))
                    run_round(gens)
            else:
                for r_ in range(sb_lo * TT, sb_hi * TT + 1):
                    gens = []
                    if r_ >= sb_lo * TT + 1:
                        gens.append((S34(r_ - 1), 1))
                    if r_ < sb_hi * TT:
                        gens.append((S2(r_), 1))
                    if r_ % TT == TT - 1 and (r_ + 1) // TT < sb_hi:
                        gens.append((S1((r_ + 1) // TT), 1))
                    run_round(gens)
            if os.environ.get("KDEBUG_SBUF"):
                print("MODEL phase1 end us", {k: round(v, 1) for k, v in P.efree.items()}, flush=True)
            barrier()

        if "nomlp" not in debug:
            with ExitStack() as s2:
                W1 = SB(s2, "W1", [128, 8, 4 * D], BF16)
                W2 = SB(s2, "W2", [128, 32, D], BF16)
                stg2 = [SB(s2, f"fstg{i}", [128, 2048]) for i in range(2)]
                hh2 = [[SB(s2, f"hh{c}_{i}", [128, D]) for i in range(TT)] for c in range(2)]
                nb = [SB(s2, f"nb{i}", [128, D], BF16) for i in range(TT)]
                nT2 = [SB(s2, f"nT{c}", [128, 8, SBT], BF16) for c in range(2)]
                aT = SB(s2, "aT", [128, 32, SBT], BF16)
                rz = [SB(s2, f"rz{i}", [128, SBT]) for i in range(2)]
                obuf = [SB(s2, f"obuf{i}", [128, D]) for i in range(2)]
                junk2 = SB(s2, "junk2", [128, D], BF16)
                fss = [SB(s2, f"fss{i}", [128, 4]) for i in range(2)]
                engs = ["dve", "pool"]
                ci = 0
                for kt in range(8):
                    for hf in range(2):
                        st_ = stg2[ci % 2]
                        dma(st_[:], w_ff1[kt * 128:(kt + 1) * 128, hf * 2048:(hf + 1) * 2048])
                        scale_cast(ci, W1[:, kt, hf * 2048:(hf + 1) * 2048], st_[:], fg[:, kt:kt + 1])
                        ci += 1
                for f2 in range(16):
                    st_ = stg2[ci % 2]
                    dma(st_[:].rearrange("p (a n) -> p a n", a=2), w_ff2[f2 * 256:(f2 + 1) * 256, :].rearrange("(a p) n -> p a n", p=128))
                    cp(["dve", "act"][ci % 2], W2[:, 2 * f2:2 * f2 + 2, :], st_[:].rearrange("p (a n) -> p a n", a=2))
                    ci += 1
                pcnt2 = [0]

                def npz():
                    pcnt2[0] += 1
                    return psb[pcnt2[0] % 4]
                def prologue(ob):
                    hh_, nT = hh2[ob % 2], nT2[ob % 2]
                    for tt in range(TT):
                        row0 = ob * SBT + tt * 128
                        fs = fss[tt % 2]
                        dma(hh_[tt][:], out_d[row0:row0 + 128, :], rk=("dout", row0))
                        act(junk2[:], hh_[tt][:], AF.Square, accum=fs[:, 0:1])
                        rstd_(fs[:, 2:3], fs[:, 0:1], 1.0 / D, EPS)
                        scale_cast(tt, nb[tt][:], hh_[tt][:], fs[:, 2:3])
                    for kt in range(8):
                        for tt in range(TT):
                            tr(psT[:, tt * 128:(tt + 1) * 128], nb[tt][:, kt * 128:(kt + 1) * 128], ident_bf[:])
                        cp("act" if kt % 2 else "dve", nT[:, kt, :], psT[:, 0:SBT])

                prologue(0)
                for ob in range(nown):
                    hh_, nT = hh2[ob % 2], nT2[ob % 2]
                    for ft in range(32):
                        pz = npz()
                        for kt in range(8):
                            mm(pz[:, 0:SBT], W1[:, kt, ft * 128:(ft + 1) * 128], nT[:, kt, :], start=(kt == 0), stop=(kt == 7))
                        rz_ = rz[ft % 2]
                        act(rz_[:], pz[:, 0:SBT], AF.Relu)
                        tten("dve", aT[:, ft, :], rz_[:], rz_[:], ALU.mult)
                    if ob + 1 < nown:
                        prologue(ob + 1)
                    for tt in range(TT):
                        row0 = ob * SBT + tt * 128
                        ob_ = obuf[tt % 2]
                        for half in range(2):
                            po = psb[4 + half]
                            for ft in range(32):
                                mm(po[:, 0:512], aT[:, ft, tt * 128:(tt + 1) * 128], W2[:, ft, half * 512:(half + 1) * 512],
                                   start=(ft == 0), stop=(ft == 31))
                            tten("dve", ob_[:, half * 512:(half + 1) * 512], po[:, 0:512], hh_[tt][:, half * 512:(half + 1) * 512], ALU.add)
                        dma(out_d[row0:row0 + 128, :], ob_[:], wk=("dout", row0))
        P.finish()
        P.emit(nc)
    return nc, dbg_out


def _consts():
    i = np.arange(128)
    ident = np.eye(128, dtype=np.float32)
    ones = np.ones((128, 128), np.float32)
    tri = (i[:, None] <= i[None, :]).astype(np.float32)
    blk = ((i[:, None] // 64) == (i[None, :] // 64)).astype(np.float32)
    maskP = np.where(i[None, :] < i[:, None], 0.0, BIG).astype(np.float32)
    maskN = np.where(i[None, :] >= i[:, None], 0.0, -BIG).astype(np.float32)
    return np.ascontiguousarray(np.stack([ident, ones, tri, blk, maskP, maskN], axis=1))


def _bias_table(rel_bias):
    ext = np.concatenate([rel_bias, np.full((8, 1), -BIG, np.float32)], axis=1)
    p = np.arange(128)[:, None, None]
    r = np.arange(5)[None, :, None]
    q = np.arange(128)[None, None, :]
    k = 128 * r + p
    rel = np.clip(512 + q - k, -256, 256) + 256
    kc, qc = k // 64, q // 64
    valid = (kc >= qc) & (kc <= qc + 8)
    idx = np.where(valid, rel, 513)
    tab = ext[:, idx]
    tab = np.transpose(tab, (1, 2, 0, 3))
    return np.ascontiguousarray(tab.reshape(128, 5 * 8 * 128)).astype(np.float32)


def _small(inp, j):
    s = np.zeros((128, 96), np.float32)
    s[:, 0:8] = inp["mix_norm_gain"][0].reshape(8, 128).T
    s[:, 8:12] = inp["att_out_gain"][0].reshape(4, 128).T
    s[:, 12:16] = np.repeat(inp["dn_out_gain"][0][:, None], 4, axis=1)
    s[:, 16:24] = inp["ffn_norm_gain"][0].reshape(8, 128).T
    cwt = inp["dn_conv_w"][0]
    s[:, 24:72] = np.transpose(cwt.reshape(4, 12, 128), (2, 1, 0)).reshape(128, 48)
    s[:, 72] = np.tile(inp["att_q_gain"][0], 2)
    s[:, 73] = np.tile(inp["att_k_gain"][0], 2)
    s[:, 74:78] = np.broadcast_to(inp["dn_dt_bias"][0][None, :], (128, 4))
    s[:, 78:82] = np.broadcast_to(inp["dn_a_log"][0][None, :], (128, 4))
    if j == 0:
        s[:, 82] = np.float32(-BIG)
    return s


_CACHE = {}


def kernel(**inputs):
    inp = {k: np.asarray(v, dtype=np.float32) for k, v in inputs.items()}
    x = inp["x"]
    B, L, _ = x.shape
    nq = 4
    own_tok = L // nq
    nown = own_tok // SBT
    npre = (L - own_tok) // SBT
    key = (npre, nown)
    if key not in _CACHE:
        _CACHE[key] = build_program(npre, nown)[0]
    nc = _CACHE[key]
    cst = _consts()
    bias = _bias_table(inp["rel_bias"][0])
    in_maps = []
    for c in range(8):
        b, j = c // nq, c % nq
        xsn = np.zeros(((npre + nown) * SBT, D), np.float32)
        n_real = own_tok * (j + 1)
        xsn[xsn.shape[0] - n_real:, :] = x[b, :n_real, :]
        in_maps.append({
            "xs": xsn, "w_in": inp["w_in"][0], "w_out": inp["w_out"][0], "w_ff1": inp["w_ff1"][0],
            "w_ff2": inp["w_ff2"][0], "cst": cst, "biasT": bias, "small": _small(inp, j),
        })
    res = run_bass_kernel_spmd(nc, in_maps, core_ids=list(range(8)))
    out = np.zeros((B, L, D), np.float32)
    for c in range(8):
        b, j = c // nq, c % nq
        out[b, j * own_tok:(j + 1) * own_tok, :] = res.results[c]["out"]
    return out
```

```python
import os
from contextlib import ExitStack

import numpy as np
import concourse.bass as bass
import concourse.mybir as mybir
from concourse.bass_utils import run_bass_kernel_spmd

F32 = mybir.dt.float32
BF16 = mybir.dt.bfloat16
AF = mybir.ActivationFunctionType
ALU = mybir.AluOpType

D = 1024
SBT = 256
TT = SBT // 128
EPS = 1e-6
BIG = 30000.0
INC = 3592


class Prog:
    ENG = ("pe", "act", "dve", "pool", "sp")
    M = 30000
    K = 8

    def __init__(self):
        self.ops = {e: [] for e in self.ENG}
        self.ndma = {e: 0 for e in self.ENG}
        self.waited = {e: {} for e in self.ENG}
        self.state = {}
        self.efree = {e: 0.0 for e in self.ENG}
        self.tfin = {}
        self.hop_fin = 0.0

    @staticmethod
    def _nm(a):
        if isinstance(a, str):
            return a
        t = getattr(a, "tensor", None)
        return t.name if t is not None else a.name

    @classmethod
    def _key(cls, x):
        if isinstance(x, tuple):
            return (cls._nm(x[0]), x[1])
        return (cls._nm(x), None)

    def _st(self, name):
        st = self.state.get(name)
        if st is None:
            st = {"w": {}, "r": {}}
            self.state[name] = st
        return st

    def op(self, eng, fn, r=(), w=(), dma=False, dur=0.3):
        r2, w2 = [], []
        for x in r:
            if x is None or isinstance(x, (int, float)):
                continue
            k = self._key(x)
            if k[0].startswith("ps"):
                w2.append((k[0], None))
            else:
                r2.append(k)
        for x in w:
            k = self._key(x)
            w2.append((k[0], None) if k[0].startswith("ps") else k)
        r, w = r2, w2
        deps = set()
        idx = len(self.ops[eng])
        if dma:
            n = self.ndma[eng]
            self.ndma[eng] += 1
            tok = ("d", eng, n)
            if n >= self.K:
                deps.add(("d", eng, n - self.K))
        else:
            tok = ("c", eng, idx)
        for (name, sub) in r:
            st = self._st(name)
            subs = [sub, None] if sub is not None else list(st["w"].keys())
            for s in subs:
                t = st["w"].get(s)
                if t is not None:
                    deps.add(t)
        for (name, sub) in w:
            st = self._st(name)
            subs = [sub, None] if sub is not None else list(set(st["w"].keys()) | set(st["r"].keys()))
            for s in subs:
                t = st["w"].get(s)
                if t is not None:
                    deps.add(t)
                for t in st["r"].get(s, {}).values():
                    deps.add(t)
        for (name, sub) in r:
            st = self._st(name)
            rk = (tok[0], tok[1]) if tok[0] == "c" else (tok[0], tok[1], tok[2] % self.K)
            st["r"].setdefault(sub, {})[rk] = tok
        for (name, sub) in w:
            st = self._st(name)
            if sub is None:
                st["w"] = {None: tok}
                st["r"] = {}
            else:
                st["w"][sub] = tok
                st["r"][sub] = {}
        t_start = self.efree[eng]
        for t in deps:
            if t != tok:
                t_start = max(t_start, self.tfin.get(t, 0.0) + (0.1 if (t[0] == "c" and t[1] == eng) else 0.35))
        if dma:
            self.efree[eng] = t_start + 0.1
            t_end = t_start + 2.0 + dur
        else:
            t_end = t_start + dur
            self.efree[eng] = t_end
        self.tfin[tok] = t_end
        self.hop_fin = max(self.hop_fin, t_end)
        waits = {}
        for t in deps:
            if t == tok:
                continue
            if t[0] == "c":
                if t[1] == eng and eng == "pe" and not dma:
                    continue
                wk = ("c", t[1])
            else:
                wk = ("d", t[1], t[2] % self.K)
            if self.waited[eng].get(wk, -1) >= t[2]:
                continue
            if wk not in waits or waits[wk][2] < t[2]:
                waits[wk] = t
        for wk, t in waits.items():
            self.waited[eng][wk] = t[2]
            if t[0] == "c":
                self.ops[t[1]][t[2]]["needed"] = True
        self.ops[eng].append({"fn": fn, "waits": list(waits.values()), "needed": False, "dma": dma, "tok": tok})
        return tok

    def dma_tail_waits(self):
        waits = []
        for e in self.ENG:
            n = self.ndma[e]
            for slot in range(min(n, self.K)):
                last = ((n - 1 - slot) // self.K) * self.K + slot
                waits.append(("d", e, last))
        return waits

    def finish(self):
        self.ops["sp"].append({"fn": None, "waits": self.dma_tail_waits(), "needed": False, "dma": False, "tok": None})

    def emit(self, nc):
        semnames = []
        seen = set()

        def sem(name):
            if name not in seen:
                seen.add(name)
                semnames.append(name)
            return name

        for e in self.ENG:
            c = 0
            for o in self.ops[e]:
                if o["dma"]:
                    n = o["tok"][2]
                    o["inc"] = (sem(f"d{e}{n % self.K}"), 16)
                elif o["needed"]:
                    c += 1
                    k = (c - 1) // self.M
                    o["inc"] = (sem(f"{e}{k}"), 1)
                    o["val"] = c - k * self.M
                else:
                    o["inc"] = None

        def resolve(t):
            if t[0] == "d":
                return (f"d{t[1]}{t[2] % self.K}", 16 * (t[2] // self.K + 1))
            o = self.ops[t[1]][t[2]]
            return (o["inc"][0], o["val"])

        with ExitStack() as stack:
            sems = {name: stack.enter_context(nc.semaphore(name)) for name in semnames}
            with nc.Block() as block:
                def mk(eng):
                    def body(e):
                        for o in self.ops[eng]:
                            for t in o["waits"]:
                                s, v = resolve(t)
                                e.wait_ge(sems[s], v)
                            if o["fn"] is not None:
                                ins = o["fn"](e)
                                if o["inc"] is not None:
                                    ins.then_inc(sems[o["inc"][0]], o["inc"][1])
                    return body
                block.tensor(mk("pe"))
                block.scalar(mk("act"))
                block.vector(mk("dve"))
                block.gpsimd(mk("pool"))
                block.sync(mk("sp"))


def build_program(npre, nown, debug=None):
    NS = npre + nown
    NTOK = NS * SBT
    debug = debug or ()
    nc = bass.Bass("TRN2", target_bir_lowering=False)
    dr = lambda name, shape, kind="ExternalInput": nc.dram_tensor(name, shape, F32, kind=kind).ap()
    xs = dr("xs", [NTOK, D])
    w_in = dr("w_in", [D, INC])
    w_out = dr("w_out", [D, D])
    w_ff1 = dr("w_ff1", [D, 4 * D])
    w_ff2 = dr("w_ff2", [4 * D, D])
    cst = dr("cst", [128, 6, 128])
    bias_d = dr("biasT", [128, 5 * 8 * 128])
    small = dr("small", [128, 96])
    out_d = dr("out", [nown * SBT, D], kind="ExternalOutput")
    dbg_out = {}

    P = Prog()
    root = ExitStack()
    with root:
        sb_bytes = {}
        sfx = [""]

        def SB(stack, name, shape, dt=F32):
            name = name + sfx[0]
            n = 1
            for d_ in shape[1:]:
                n *= d_
            sb_bytes[name] = n * (2 if dt == BF16 else 4)
            if os.environ.get("KDEBUG_SBUF"):
                print("SBUF", name, sb_bytes[name], flush=True)
            return stack.enter_context(nc.sbuf_tensor(name, shape, dt))

        def PS(stack, name, shape, dt=F32):
            return stack.enter_context(nc.psum_tensor(name, shape, dt))

        def fsz(ap):
            n = 1
            for d_ in ap.shape[1:]:
                n *= d_
            return n

        def mm(out, lhsT, rhs, start=True, stop=True, skip=False):
            n = fsz(rhs)
            d_ = 0.12 + n * 0.0003
            if lhsT.dtype == F32:
                d_ *= 2.0
            P.op("pe", lambda e: e.matmul(out, lhsT=lhsT, rhs=rhs, start=start, stop=stop, skip_group_check=skip),
                 r=[lhsT, rhs], w=[out], dur=d_)

        def tr(out, in_, ident):
            P.op("pe", lambda e: e.transpose(out=out, in_=in_, identity=ident), r=[in_, ident], w=[out],
                 dur=0.3 if in_.dtype == F32 else 0.16)

        def act(out, in_, func, bias=0.0, scale=1.0, accum=None):
            d_ = 0.2 + fsz(out) * 0.00095
            if accum is None:
                P.op("act", lambda e: e.activation(out=out, in_=in_, func=func, bias=bias, scale=scale),
                     r=[in_, bias, scale], w=[out], dur=d_)
            else:
                P.op("act", lambda e: e.activation(out=out, in_=in_, func=func, bias=bias, scale=scale, accum_out=accum),
                     r=[in_, bias, scale], w=[out, accum], dur=d_ + 0.1)

        def edur(eng, out):
            return (0.1 + fsz(out) * 0.0023) if eng == "pool" else (0.1 + fsz(out) * 0.0012)

        def tten(eng, out, in0, in1, op):
            P.op(eng, lambda e: e.tensor_tensor(out=out, in0=in0, in1=in1, op=op), r=[in0, in1], w=[out], dur=edur(eng, out))

        def ts(eng, out, in0, s1, s2, op0, op1=None):
            if op1 is None:
                P.op(eng, lambda e: e.tensor_scalar(out=out, in0=in0, scalar1=s1, scalar2=None, op0=op0), r=[in0, s1], w=[out],
                     dur=edur(eng, out))
            else:
                P.op(eng, lambda e: e.tensor_scalar(out=out, in0=in0, scalar1=s1, scalar2=s2, op0=op0, op1=op1),
                     r=[in0, s1, s2], w=[out], dur=edur(eng, out))

        def stt(out, in0, scalar, in1, op0, op1):
            P.op("dve", lambda e: e.scalar_tensor_tensor(out=out, in0=in0, scalar=scalar, in1=in1, op0=op0, op1=op1),
                 r=[in0, scalar, in1], w=[out], dur=0.2 + fsz(out) * 0.0012)

        def cp(eng, out, in_):
            if eng == "act":
                act(out, in_, AF.Copy)
            else:
                P.op(eng, lambda e: e.tensor_copy(out=out, in_=in_), r=[in_], w=[out], dur=edur(eng, out))

        def recip(out, in_):
            P.op("dve", lambda e: e.reciprocal(out=out, in_=in_), r=[in_], w=[out], dur=0.1 + fsz(out) * 0.008)

        def rstd_(out, in_, scale, bias):
            act(out, in_, AF.Ln, bias=bias, scale=scale)
            act(out, out, AF.Exp, scale=-0.5)

        def silu_(out, in_, tmp):
            act(tmp, in_, AF.Exp, scale=-1.0)
            act(tmp, tmp, AF.Ln, bias=1.0)
            act(tmp, tmp, AF.Exp, scale=-1.0)
            tten("dve" if in_.tensor.name.startswith("ps") else "pool", out, in_, tmp, ALU.mult)

        def scale_cast(i, out, in_, sc):
            if i % 2 == 0:
                ts("dve", out, in_, sc, None, ALU.mult)
            else:
                act(out, in_, AF.Copy, scale=sc)

        def memset(eng, ap, val):
            P.op(eng, lambda e: e.memset(ap, val), w=[ap])

        def dma(out, in_, rk=None, wk=None):
            P.op("sp", lambda e: e.dma_start(out=out, in_=in_), r=[rk if rk is not None else in_],
                 w=[wk if wk is not None else out], dma=True, dur=fsz(out) * 0.0004)

        def dump(name, src, shape):
            if name not in debug:
                return
            d = nc.dram_tensor("dbg_" + name, shape, F32, kind="ExternalOutput").ap()
            dbg_out[name] = d
            dma(d, src)

        ident_f = SB(root, "ident_f", [128, 128])
        ones_f = SB(root, "ones_f", [128, 128])
        tri_f = SB(root, "tri_f", [128, 128])
        maskP = SB(root, "maskP", [128, 128])
        maskN = SB(root, "maskN", [128, 128])
        ident_bf = SB(root, "ident_bf", [128, 128], BF16)
        ones_bf = SB(root, "ones_bf", [128, 128], BF16)
        blk_bf = SB(root, "blk_bf", [128, 128], BF16)
        smallt = SB(root, "smallt", [128, 96])
        bar_a = SB(root, "bar_a", [128, 4])
        bar_s = SB(root, "bar_s", [128, 8])
        psb = [PS(root, f"ps{i}", [128, 512]) for i in range(7)]
        psT = PS(root, "psT", [128, 1024], BF16)

        def barrier():
            P.ops["sp"].append({"fn": None, "waits": P.dma_tail_waits(), "needed": False, "dma": False, "tok": None})
            P.op("sp", lambda e: e.dma_start(out=bar_s[:, 4:8], in_=bar_s[:, 0:4]), r=[bar_s], w=[("BAR", "sp"), bar_s], dma=True)
            P.op("pe", lambda e: e.matmul(psb[6][0:1, 0:1], lhsT=ident_bf[:, 0:1], rhs=ident_bf[:, 0:1], start=True, stop=True),
                 r=[ident_bf], w=[("BAR", "pe"), psb[6]])
            P.op("act", lambda e: e.activation(out=bar_a[:, 0:1], in_=ident_f[:, 0:1], func=AF.Copy), r=[ident_f], w=[("BAR", "act")])
            P.op("dve", lambda e: e.memset(bar_a[:, 1:2], 0.0), w=[("BAR", "dve")])
            P.op("pool", lambda e: e.memset(bar_a[:, 2:3], 0.0), w=[("BAR", "pool")])
            P.op("pe", lambda e: e.matmul(psb[6][0:1, 0:1], lhsT=ident_bf[:, 0:1], rhs=ident_bf[:, 0:1], start=True, stop=True),
                 r=["BAR", ident_bf], w=[psb[6]])
            P.op("act", lambda e: e.activation(out=bar_a[:, 0:1], in_=ident_f[:, 0:1], func=AF.Copy), r=["BAR", ident_f])
            P.op("dve", lambda e: e.memset(bar_a[:, 1:2], 0.0), r=["BAR"])
            P.op("pool", lambda e: e.memset(bar_a[:, 2:3], 0.0), r=["BAR"])
            P.op("sp", lambda e: e.dma_start(out=bar_s[:, 4:8], in_=bar_s[:, 0:4]), r=["BAR", bar_s], w=[("BAR2", "sp"), bar_s], dma=True)

        with ExitStack() as s0:
            cst_f = SB(s0, "cst_f", [128, 6, 128])
            dma(cst_f[:], cst)
            dma(smallt[:], small)
            cp("dve", ident_f[:], cst_f[:, 0, :])
            cp("dve", ones_f[:], cst_f[:, 1, :])
            cp("dve", tri_f[:], cst_f[:, 2, :])
            cp("dve", maskP[:], cst_f[:, 4, :])
            cp("dve", maskN[:], cst_f[:, 5, :])
            cp("pool", ident_bf[:], cst_f[:, 0, :])
            cp("pool", ones_bf[:], cst_f[:, 1, :])
            cp("pool", blk_bf[:], cst_f[:, 3, :])
            memset("pool", bar_s[:], 0.0)
            barrier()
        mg = smallt[:, 0:8]
        wog = smallt[:, 8:16]
        fg = smallt[:, 16:24]
        cw = smallt[:, 24:72]
        qg = smallt[:, 72:73]
        kg = smallt[:, 73:74]
        dtb = smallt[:, 74:78]
        alog = smallt[:, 78:82]
        hmask = smallt[:, 82:83]
        nexpA = smallt[:, 84:88]
        qg8 = smallt[:, 88:89]

        halo = SB(root, "halo", [128, 12, 3])
        Sst = SB(root, "Sst", [128, 4, 128])
        Sbf = SB(root, "Sbf", [128, 4, 128], BF16)
        memset("pool", halo[:], 0.0)
        memset("pool", Sst[:], 0.0)
        memset("pool", Sbf[:], 0.0)
        act(nexpA, alog, AF.Exp)
        ts("dve", nexpA, nexpA, -1.0, None, ALU.mult)
        ts("dve", qg8, qg, 0.125, None, ALU.mult)
        phases = [(0, npre - 2, True), (npre - 2, NS, False)] if npre - 2 >= 2 else [(0, NS, False)]
        for ph, (sb_lo, sb_hi, deep) in enumerate(phases):
          with ExitStack() as s1:
            sfx[0] = f"_{ph}"
            NA = 3 if deep else 2
            NBS = 4 if deep else 2
            NS2 = 2 if deep else 1
            if deep:
                Win = SB(s1, "Win", [128, 8, 1544], BF16)

                def Wc(kt, c0, c1, Win=Win):
                    o = 1536 if c0 < 3584 else 3584 - 1536
                    return Win[:, kt, c0 - o:c1 - o]
            else:
                Win = SB(s1, "Win", [128, 8, INC], BF16)
                Wout = SB(s1, "Wout", [128, 8, D], BF16)
                biasT = SB(s1, "biasTb", [128, 5, 8, 128], BF16)

                def Wc(kt, c0, c1, Win=Win):
                    return Win[:, kt, c0:c1]
            with ExitStack() as s1a:
                stg = [SB(s1a, f"stg{i}", [128, INC]) for i in range(2)]
                for kt in range(8):
                    st_ = stg[kt % 2]
                    if deep:
                        dma(st_[:, 0:1536], w_in[kt * 128:(kt + 1) * 128, 1536:3072])
                        dma(st_[:, 1536:1544], w_in[kt * 128:(kt + 1) * 128, 3584:3592])
                        scale_cast(kt, Win[:, kt, :], st_[:, 0:1544], mg[:, kt:kt + 1])
                    else:
                        dma(st_[:], w_in[kt * 128:(kt + 1) * 128, :])
                        scale_cast(kt, Win[:, kt, :], st_[:], mg[:, kt:kt + 1])
                if not deep:
                    for kt in range(8):
                        st_ = stg[kt % 2]
                        dma(st_[:, 0:D], w_out[kt * 128:(kt + 1) * 128, :])
                        scale_cast(kt, Wout[:, kt, :], st_[:, 0:D], wog[:, kt:kt + 1])
                    for r_ in range(5):
                        st_ = stg[r_ % 2]
                        dma(st_[:, 0:1024], bias_d[:, r_ * 1024:(r_ + 1) * 1024])
                        cp(["dve", "act"][r_ % 2], biasT[:, r_, :, :], st_[:, 0:1024].rearrange("p (h q) -> p h q", h=8))
                barrier()

            xt = [SB(s1, f"xt{i}", [128, D]) for i in range(3 if deep else 1)]
            nss = [SB(s1, f"nss{i}", [128, 4]) for i in range(3)]
            ub = [SB(s1, f"ub{i}", [128, D], BF16) for i in range(TT)]
            prw = [SB(s1, f"prw{i}", [128, SBT + 3]) for i in range(3)]
            cv = [SB(s1, f"cv{i}", [128, SBT]) for i in range(3)]
            cs = [SB(s1, f"cs{i}", [128, SBT]) for i in range(3)]
            sq = [SB(s1, f"sq{i}", [128, SBT], BF16) for i in range(4)]
            sd2 = [None, None, None, SB(s1, "sd23", [128, SBT])]
            kraw = [SB(s1, "kraw0", [128, SBT])]
            uT = [SB(s1, f"uT{a}", [128, 8, SBT], BF16) for a in range(NA)]
            dqT = [[SB(s1, f"dqT{a}_{h}", [128, SBT], BF16) for h in range(4)] for a in range(NA)]
            dkT = [[SB(s1, f"dkT{a}_{h}", [128, SBT], BF16) for h in range(4)] for a in range(NA)]
            dvT = [SB(s1, f"dvT{h}", [128, SBT], BF16) for h in range(4)]
            Ktm = [SB(s1, f"Ktm{a}", [128, 4, TT, 128], BF16) for a in range(NA)]
            Vtm = [SB(s1, f"Vtm{a}", [128, 4, TT, 128], BF16) for a in range(NA)]
            ab = [SB(s1, f"ab{i}", [128, 8]) for i in range(TT)]
            gsm = [[SB(s1, f"gsm{a}_{i}", [128, 24]) for i in range(TT)] for a in range(NA)]
            Gb_ = [SB(s1, f"Gb{k}", [128, 4, 128]) for k in range(NS2)]
            Lm_ = [SB(s1, f"Lm{k}", [128, 4, 128]) for k in range(NS2)]
            LTm_ = Gb_
            At_ = [[SB(s1, f"At{k}_{i}", [128, 4, 128], BF16) for i in range(2)] for k in range(NS2)]
            Bt_ = [[SB(s1, f"Bt{k}_{i}", [128, 4, 128], BF16) for i in range(2)] for k in range(NS2)]
            XB = [SB(s1, f"XB{i}", [128, 4, 128], BF16) for i in range(NBS)]
            TTb = [SB(s1, f"TTb{i}", [128, 4, 128], BF16) for i in range(NBS)]
            PTd = [SB(s1, f"PTd{i}", [128, 4, 128], BF16) for i in range(NBS)]
            Vb = [SB(s1, f"Vb{i}", [128, 4, 128]) for i in range(NBS)]
            Kd = [SB(s1, f"Kd{i}", [128, 4, 128], BF16) for i in range(NBS)]
            dsm = [SB(s1, f"dsm{i}", [128, 32]) for i in range(NBS)]
            Rr = SB(s1, "Rr", [128, 4, 128], BF16)
            Vn0 = SB(s1, "Vn0", [128, 4, 128], BF16)
            rres = SB(s1, "rres", [128, 4, 128], BF16)
            Vn = SB(s1, "Vn", [128, 4, 128], BF16)
            O1 = SB(s1, "O1", [128, 4, 128])
            Odn = SB(s1, "Odn", [128, 4, 128])
            if not deep:
                NR = 4
                akT = [SB(s1, f"akT{i}", [128, 4, SBT], BF16) for i in range(NR)]
                vaug = [SB(s1, f"vaug{i}", [128, TT, 8, 65], BF16) for i in range(NR)]
                aqTz = SB(s1, "aqTz", [128, 4, 2, 128], BF16)
                sg = SB(s1, "sg", [128, 512])
                PTa = [SB(s1, f"PTa{i}", [128, 512], BF16) for i in range(2)]
                oatt = SB(s1, "oatt", [128, 512])
                rec = SB(s1, "rec", [128, 16])
                cat = SB(s1, "cat", [128, D], BF16)
                catT = SB(s1, "catT", [128, D], BF16)
                xres_ = SB(s1, "xres", [128, D])
                qraw_ = SB(s1, "qraw4", [128, 512])
                qsd_ = SB(s1, "qsd4", [128, 512])

            if not deep:
                memset("pool", aqTz[:], 0.0)
                for i in range(NR):
                    memset("pool", vaug[i][:], 1.0)

            pcnt = [0]

            def next_pp():
                pcnt[0] += 1
                return psb[pcnt[0] % 2]
            dcnt = [0]

            pd_sets = [[2, 3], [5, 6]] if deep else [[2, 3]]
            dcnt2 = [0, 0]

            def next_pd_of(k2):
                def f():
                    dcnt2[k2] += 1
                    return psb[pd_sets[k2][dcnt2[k2] % 2]]
                return f
            pS3 = psb[4]

            def S1(sb):
                A = sb % NA
                att = sb >= npre - 2
                for tt in range(TT):
                    x_ = xt[(sb * TT + tt) % len(xt)]
                    ns_ = nss[tt]
                    dma(x_[:], xs[sb * SBT + tt * 128: sb * SBT + (tt + 1) * 128, :])
                    act(ub[tt][:], x_[:], AF.Square, accum=ns_[:, 0:1])
                    rstd_(ns_[:, 2:3], ns_[:, 0:1], 1.0 / D, EPS)
                    scale_cast(tt, ub[tt][:], x_[:], ns_[:, 2:3])
                    yield
                for kt in range(8):
                    for tt in range(TT):
                        tr(psT[:, tt * 128:(tt + 1) * 128], ub[tt][:, kt * 128:(kt + 1) * 128], ident_bf[:])
                    cp("act" if kt % 2 else "dve", uT[A][:, kt, :], psT[:, 0:SBT])
                    if kt % 2:
                        yield
                groups = [S1grp(sb, 1), S1grp(sb, 2), S1ab(sb)]
                if sb >= npre - 1:
                    groups.insert(0, S1grp(sb, 0))
                yield ("spawn", groups)

            def S1grp(sb, grp):
                A = sb % NA
                pw, cv_, cs_, sq_, sd_ = prw[grp], cv[grp], cs[grp], sq[grp], cv[grp]
                for h in range(4):
                    ft = 4 * grp + h
                    pp = next_pp()
                    c0 = 1536 + 128 * ft
                    for kt in range(8):
                        mm(pp[:, 0:SBT], Wc(kt, c0, c0 + 128), uT[A][:, kt, :], start=(kt == 0), stop=(kt == 7))
                    cp("act", pw[:, 3:3 + SBT], pp[:, 0:SBT])
                    P.op("pool", lambda e, pw=pw, ft=ft: e.tensor_copy(out=pw[:, 0:3], in_=halo[:, ft, :]), r=[(halo, ft)], w=[pw])
                    yield
                    ts("dve", cv_[:], pw[:, 0:SBT], cw[:, ft * 4:ft * 4 + 1], None, ALU.mult)
                    for j in range(1, 4):
                        stt(cv_[:], pw[:, j:j + SBT], cw[:, ft * 4 + j:ft * 4 + j + 1], cv_[:], ALU.mult, ALU.add)
                    P.op("pool", lambda e, pw=pw, ft=ft: e.tensor_copy(out=halo[:, ft, :], in_=pw[:, SBT:SBT + 3]), r=[pw], w=[(halo, ft)])
                    silu_(cs_[:], cv_[:], cs_[:])
                    yield
                    if grp < 2:
                        tten("pool", sq_[:], cs_[:], cs_[:], ALU.mult)
                        pn_ = next_pp()
                        mm(pn_[:, 0:SBT], ones_bf[:], sq_[:])
                        sc = 128.0 if grp == 0 else 1.0
                        rstd_(sd_[:], pn_[:, 0:SBT], sc, EPS * sc)
                        tten("dve", (dqT if grp == 0 else dkT)[A][h][:], cs_[:], sd_[:], ALU.mult)
                    else:
                        cp("pool", dvT[h][:], cs_[:])
                    yield
                if grp > 0:
                    srcT, dst = (dkT[A], Ktm[A]) if grp == 1 else (dvT, Vtm[A])
                    for h in range(4):
                        for tt in range(TT):
                            tr(psT[:, (h * TT + tt) * 128:(h * TT + tt + 1) * 128], srcT[h][:, tt * 128:(tt + 1) * 128], ident_bf[:])
                    cp("act", dst[:].rearrange("p h t d -> p (h t d)"), psT[:, 0:4 * TT * 128])
                    yield

            def S1ab(sb):
                A = sb % NA
                att = sb >= npre - 2
                for tt in range(TT):
                    pn_ = next_pp()
                    for kt in range(8):
                        mm(pn_[:, 0:8], uT[A][:, kt, tt * 128:(tt + 1) * 128], Wc(kt, 3584, 3592), start=(kt == 0), stop=(kt == 7))
                    cp("dve", ab[tt][:], pn_[:, 0:8])
                    gs = gsm[A][tt]
                    tten("dve", gs[:, 0:4], ab[tt][:, 0:4], dtb, ALU.add)
                    act(gs[:, 4:8], gs[:, 0:4], AF.Exp)
                    act(gs[:, 8:12], gs[:, 4:8], AF.Ln, bias=1.0)
                    tten("dve", gs[:, 12:16], gs[:, 8:12], nexpA, ALU.mult)
                    act(gs[:, 16:20], ab[tt][:, 4:8], AF.Exp, scale=-1.0)
                    act(gs[:, 16:20], gs[:, 16:20], AF.Ln, bias=1.0)
                    act(gs[:, 16:20], gs[:, 16:20], AF.Exp, scale=-1.0)
                    ts("dve", gs[:, 20:24], gs[:, 16:20], -1.0, None, ALU.mult)
                    yield
                if att:
                    slot = sb % NR
                    kr, sq_, sd_ = kraw[0], sq[3], sd2[3]
                    for p_ in range(4):
                        pp = next_pp()
                        c0 = 512 + 128 * p_
                        for kt in range(8):
                            mm(pp[:, 0:SBT], Wc(kt, c0, c0 + 128), uT[A][:, kt, :], start=(kt == 0), stop=(kt == 7))
                        cp("act", kr[:], pp[:, 0:SBT])
                        tten("pool", sq_[:], kr[:], kr[:], ALU.mult)
                        yield
                        pn_ = next_pp()
                        mm(pn_[:, 0:SBT], blk_bf[:], sq_[:])
                        rstd_(sd_[:], pn_[:, 0:SBT], 1.0 / 64, EPS)
                        stt(akT[slot][:, p_, :], kr[:], kg, sd_[:], ALU.mult, ALU.mult)
                        yield
                    for tt in range(TT):
                        pp = next_pp()
                        for kt in range(8):
                            mm(pp[:, 0:512], uT[A][:, kt, tt * 128:(tt + 1) * 128], Win[:, kt, 1024:1536], start=(kt == 0), stop=(kt == 7))
                        cp("act", vaug[slot][:, tt, :, 0:64], pp[:, 0:512].rearrange("p (h d) -> p h d", h=8))
                        yield

            def S2(blk):
                sb, tt = divmod(blk, TT)
                A = sb % NA
                Bs = blk % NBS
                own = sb >= npre
                tsl = slice(tt * 128, (tt + 1) * 128)
                k2 = blk % NS2
                next_pd = next_pd_of(k2)
                Gb, Lm, LTm, At, Bt = Gb_[k2], Lm_[k2], LTm_[k2], At_[k2], Bt_[k2]
                gs = gsm[A][tt]
                g_ = gs[:, 12:16]
                nbeta = gs[:, 20:24]
                ds_ = dsm[Bs]
                dcol, dlast, ed, nbed, dd, edl, cd = (ds_[:, 4 * i:4 * i + 4] for i in range(7))
                pc = next_pd()
                mm(pc[:, 0:4], tri_f[:], g_)
                mm(pc[:, 4:8], ones_f[:], g_)
                cp("dve", ds_[:, 0:8], pc[:, 0:8])
                tten("dve", Gb[:], ones_f[:].unsqueeze(1).to_broadcast([128, 4, 128]),
                     g_.unsqueeze(2).to_broadcast([128, 4, 128]), ALU.mult)
                yield
                act(ed, dcol, AF.Exp)
                tten("dve", nbed, ed, nbeta, ALU.mult)
                tten("dve", dd, dlast, dcol, ALU.subtract)
                act(edl, dd, AF.Exp)
                act(cd, dlast, AF.Exp)
                pD = next_pd()
                for h in range(4):
                    mm(pD[:, h * 128:(h + 1) * 128], Gb[:, h, :], tri_f[:])
                yield
                for h in range(4):
                    stt(Lm[:, h, :], pD[:, h * 128:(h + 1) * 128], dcol[:, h:h + 1], maskP[:], ALU.subtract, ALU.max)
                if own:
                    for h in range(4):
                        stt(LTm[:, h, :], pD[:, h * 128:(h + 1) * 128], dcol[:, h:h + 1], maskN[:], ALU.subtract, ALU.min)
                act(Lm[:], Lm[:], AF.Exp, scale=-1.0)
                if own:
                    act(LTm[:], LTm[:], AF.Exp)
                pK = next_pd()
                for h in range(4):
                    mm(pK[:, h * 128:(h + 1) * 128], dkT[A][h][:, tsl], dkT[A][h][:, tsl])
                yield
                A0, B0, X = At[0], Bt[0], XB[Bs]
                for h in range(4):
                    stt(A0[:, h, :], pK[:, h * 128:(h + 1) * 128], nbeta[:, h:h + 1], Lm[:, h, :], ALU.mult, ALU.mult)
                if own:
                    pQ = next_pd()
                    for h in range(4):
                        mm(pQ[:, h * 128:(h + 1) * 128], dkT[A][h][:, tsl], dqT[A][h][:, tsl])
                    tten("dve", PTd[Bs][:].rearrange("p h c -> p (h c)"), pQ[:, 0:512], LTm[:].rearrange("p h c -> p (h c)"), ALU.mult)
                yield
                pBt = next_pd()[:].bitcast(BF16)
                for h in range(4):
                    tr(pBt[:, h * 128:(h + 1) * 128], A0[:, h, :], ident_bf[:])
                cp("act", B0[:].rearrange("p h c -> p (h c)"), pBt[:, 0:512])
                tten("pool", X[:], B0[:], ident_f[:].unsqueeze(1).to_broadcast([128, 4, 128]), ALU.add)
                tten("pool", TTb[Bs][:], ident_f[:].unsqueeze(1).to_broadcast([128, 4, 128]), B0[:], ALU.subtract)
                yield
                Ap, Bp = A0, B0
                for lvl in range(1, 6):
                    An, Bn = At[lvl % 2], Bt[lvl % 2]
                    pA = next_pd()
                    for h in range(4):
                        mm(pA[:, h * 128:(h + 1) * 128], Bp[:, h, :], Ap[:, h, :])
                    cp("act", An[:].rearrange("p h c -> p (h c)"), pA[:, 0:512])
                    if lvl < 5:
                        pB = next_pd()
                        for h in range(4):
                            mm(pB[:, h * 128:(h + 1) * 128], Ap[:, h, :], Bp[:, h, :])
                        cp("dve", Bn[:].rearrange("p h c -> p (h c)"), pB[:, 0:512])
                    yield
                    pX = next_pd()
                    for h in range(4):
                        mm(pX[:, h * 128:(h + 1) * 128], An[:, h, :], X[:, h, :])
                    tten("dve", X[:].rearrange("p h c -> p (h c)"), pX[:, 0:512], X[:].rearrange("p h c -> p (h c)"), ALU.add)
                    yield
                    Ap, Bp = An, Bn
                tten("pool", Vb[Bs][:], Vtm[A][:, :, tt, :], gs[:, 16:20].unsqueeze(2).to_broadcast([128, 4, 128]), ALU.mult)
                tten("pool", Kd[Bs][:], Ktm[A][:, :, tt, :], edl.unsqueeze(2).to_broadcast([128, 4, 128]), ALU.mult)
                yield

            def S3a(blk):
                sb, tt = divmod(blk, TT)
                A = sb % NA
                Bs = blk % NBS
                own = sb >= npre
                tsl = slice(tt * 128, (tt + 1) * 128)
                ds_ = dsm[Bs]
                dcol, dlast, ed, nbed, dd, edl, cd = (ds_[:, 4 * i:4 * i + 4] for i in range(7))
                X = XB[Bs]
                for h in range(4):
                    mm(pS3[:, h * 128:(h + 1) * 128], dkT[A][h][:, tsl], Sbf[:, h, :])
                for h in range(4):
                    stt(Rr[:, h, :], pS3[:, h * 128:(h + 1) * 128], nbed[:, h:h + 1], Vb[Bs][:, h, :], ALU.mult, ALU.add)
                yield
                for h in range(4):
                    mm(pS3[:, h * 128:(h + 1) * 128], X[:, h, :], Rr[:, h, :])
                cp("act", Vn0[:].rearrange("p h c -> p (h c)"), pS3[:, 0:512])
                yield
                for h in range(4):
                    mm(pS3[:, h * 128:(h + 1) * 128], TTb[Bs][:, h, :], Vn0[:, h, :])
                stt(rres[:].rearrange("p h c -> p (h c)"), pS3[:, 0:512], -1.0, Rr[:].rearrange("p h c -> p (h c)"), ALU.mult, ALU.add)
                yield
                for h in range(4):
                    mm(pS3[:, h * 128:(h + 1) * 128], X[:, h, :], rres[:, h, :])
                tten("dve", Vn[:].rearrange("p h c -> p (h c)"), pS3[:, 0:512], Vn0[:].rearrange("p h c -> p (h c)"), ALU.add)
                yield
                if own:
                    for h in range(4):
                        mm(pS3[:, h * 128:(h + 1) * 128], dqT[A][h][:, tsl], Sbf[:, h, :])
                    for h in range(4):
                        ts("dve", O1[:, h, :], pS3[:, h * 128:(h + 1) * 128], ed[:, h:h + 1], None, ALU.mult)
                    yield
                    for h in range(4):
                        mm(pS3[:, h * 128:(h + 1) * 128], PTd[Bs][:, h, :], Vn[:, h, :])
                    tten("dve", Odn[:].rearrange("p h c -> p (h c)"), pS3[:, 0:512], O1[:].rearrange("p h c -> p (h c)"), ALU.add)
                    yield
                for h in range(4):
                    mm(pS3[:, h * 128:(h + 1) * 128], Kd[Bs][:, h, :], Vn[:, h, :])
                for h in range(4):
                    stt(Sst[:, h, :], Sst[:, h, :], cd[:, h:h + 1], pS3[:, h * 128:(h + 1) * 128], ALU.mult, ALU.add)
                cp("act", Sbf[:], Sst[:])
                yield
                if not own:
                    return
                ob = sb - npre
                row0 = ob * SBT + tt * 128
                T0 = sb * SBT + tt * 128
                if "odn" in debug:
                    dmp = dbg_out.get("odn")
                    if dmp is None:
                        dmp = nc.dram_tensor("dbg_odn", [nown * SBT, 512], F32, kind="ExternalOutput").ap()
                        dbg_out["odn"] = dmp
                    dma(dmp[row0:row0 + 128, :], Odn[:].rearrange("p h c -> p (h c)"), wk=("dbgodn", row0))
                for h in range(4):
                    act(cat[:, 512 + h * 128:512 + (h + 1) * 128], Odn[:, h, :], AF.Square, accum=rec[:, h:h + 1])
                rstd_(rec[:, 8:12], rec[:, 0:4], 1.0 / 128, EPS)
                pg = psb[4]
                for kt in range(8):
                    mm(pg[:, 0:512], uT[A][:, kt, tsl], Win[:, kt, 3072:3584], start=(kt == 0), stop=(kt == 7))
                silu_(sg[:], pg[:, 0:512], sg[:])
                for h in range(4):
                    stt(cat[:, 512 + h * 128:512 + (h + 1) * 128], Odn[:, h, :], rec[:, 8 + h:9 + h], sg[:, h * 128:(h + 1) * 128],
                        ALU.mult, ALU.mult)
                yield

            def S4att(blk):
                sb, tt = divmod(blk, TT)
                A = sb % NA
                tsl = slice(tt * 128, (tt + 1) * 128)
                ob = sb - npre
                row0 = ob * SBT + tt * 128
                T0 = sb * SBT + tt * 128
                pq, pq2 = psb[5], psb[6]
                qraw4 = qraw_[:]
                qsd4 = qsd_[:]
                qsq4 = cat[:, 0:512]
                for p_ in range(4):
                    c0 = 128 * p_
                    for kt in range(8):
                        mm(pq[:, p_ * 128:(p_ + 1) * 128], Win[:, kt, c0:c0 + 128], uT[A][:, kt, tsl], start=(kt == 0), stop=(kt == 7))
                cp("act", qraw4, pq[:, 0:512])
                tten("pool", qsq4, qraw4, qraw4, ALU.mult)
                yield
                for p_ in range(4):
                    mm(pq2[:, p_ * 128:(p_ + 1) * 128], blk_bf[:], qsq4[:, p_ * 128:(p_ + 1) * 128])
                rstd_(qsd4, pq2[:, 0:512], 1.0 / 64, EPS)
                for p_ in range(4):
                    for e_ in range(2):
                        stt(aqTz[64 * e_:64 * e_ + 64, p_, e_, :], qraw4[64 * e_:64 * e_ + 64, p_ * 128:(p_ + 1) * 128],
                            qg8[64 * e_:64 * e_ + 64, :], qsd4[64 * e_:64 * e_ + 64, p_ * 128:(p_ + 1) * 128], ALU.mult, ALU.mult)
                yield
                for hg in range(2):
                    pOa = psb[6]
                    for r_ in range(5):
                        st_tok = T0 - 512 + 128 * r_
                        sbk = st_tok // SBT
                        ttk = (st_tok % SBT) // 128
                        slk = sbk % NR
                        pSa = psb[5]
                        for hh in range(4):
                            h = 4 * hg + hh
                            p_, e_ = h // 2, h % 2
                            mm(pSa[:, hh * 128:(hh + 1) * 128], ident_bf[:], biasT[:, r_, h, :], start=True, stop=False)
                            mm(pSa[:, hh * 128:(hh + 1) * 128], akT[slk][:, p_, ttk * 128:(ttk + 1) * 128],
                               aqTz[:, p_, e_, :], start=False, stop=True)
                        pt_ = PTa[r_ % 2]
                        act(pt_[:], pSa[:, 0:512], AF.Exp, bias=(hmask if sbk < npre else 0.0))
                        for hh in range(4):
                            h = 4 * hg + hh
                            mm(pOa[:, hh * 65:hh * 65 + 65], pt_[:, hh * 128:(hh + 1) * 128], vaug[slk][:, ttk, h, :],
                               start=(r_ == 0 and hh == 0), stop=(r_ == 4), skip=True)
                        yield
                    ov = pOa[:, 0:260].rearrange("p (h c) -> p h c", c=65)
                    recip(rec[:, 12:16], ov[:, :, 64])
                    tten("dve", oatt[:, hg * 256:(hg + 1) * 256].rearrange("p (h d) -> p h d", d=64), ov[:, :, 0:64],
                         rec[:, 12:16].unsqueeze(2).to_broadcast([128, 4, 64]), ALU.mult)
                    yield
                if "oatt" in debug:
                    dmp = dbg_out.get("oatt")
                    if dmp is None:
                        dmp = nc.dram_tensor("dbg_oatt", [nown * SBT, 512], F32, kind="ExternalOutput").ap()
                        dbg_out["oatt"] = dmp
                    dma(dmp[row0:row0 + 128, :], oatt[:], wk=("dbgoatt", row0))

            def S4fin(blk):
                sb, tt = divmod(blk, TT)
                ob = sb - npre
                row0 = ob * SBT + tt * 128
                T0 = sb * SBT + tt * 128
                ns_ = nss[2]
                act(cat[:, 0:512], oatt[:], AF.Square, accum=ns_[:, 0:1])
                rstd_(ns_[:, 2:3], ns_[:, 0:1], 1.0 / 512, EPS)
                ts("dve", cat[:, 0:512], oatt[:], ns_[:, 2:3], None, ALU.mult)
                for kt in range(8):
                    tr(psT[:, kt * 128:(kt + 1) * 128], cat[:, kt * 128:(kt + 1) * 128], ident_bf[:])
                cp("act", catT[:], psT[:, 0:1024])
                dma(xres_[:], xs[T0:T0 + 128, :])
                yield
                for half in range(2):
                    pp = psb[5 + half]
                    for kt in range(8):
                        mm(pp[:, 0:512], catT[:, kt * 128:(kt + 1) * 128], Wout[:, kt, half * 512:(half + 1) * 512],
                           start=(kt == 0), stop=(kt == 7))
                    tten("dve", xres_[:, half * 512:(half + 1) * 512], pp[:, 0:512], xres_[:, half * 512:(half + 1) * 512], ALU.add)
                dma(out_d[row0:row0 + 128, :], xres_[:], wk=("dout", row0))
                yield

            def S34(blk):
                sb = blk // TT
                if sb < npre:
                    yield from S3a(blk)
                    return
                if os.environ.get("KSEQ"):
                    yield from S3a(blk)
                    yield from S4att(blk)
                    yield from S4fin(blk)
                    return
                g_att = S4att(blk)
                g_s3 = S3a(blk)
                km = os.environ.get("KMODE", "2")
                if km == "1":
                    next(g_att)
                    yield
                    next(g_att)
                    yield
                elif km == "2":
                    for _ in range(2):
                        next(g_att)
                        try:
                            next(g_s3)
                        except StopIteration:
                            pass
                        yield
                    for _ in g_s3:
                        yield
                subs = [[0.0, 0, g_s3], [0.0, 1, g_att]]
                while subs:
                    subs.sort(key=lambda t_: (t_[0], t_[1]))
                    ent = subs[0]
                    h0 = P.hop_fin
                    P.hop_fin = 0.0
                    try:
                        next(ent[2])
                    except StopIteration:
                        subs.pop(0)
                        P.hop_fin = max(h0, P.hop_fin)
                        continue
                    if P.hop_fin > 0.0:
                        ent[0] = P.hop_fin
                    P.hop_fin = max(h0, P.hop_fin)
                    yield
                yield from S4fin(blk)

            def run_round(gens):
                now = min(P.efree.values())
                live = [[now, i, g] for i, (g, w) in enumerate(gens)]
                cnt = len(live)
                while live:
                    live.sort(key=lambda t: (t[0], t[1]))
                    ent = live[0]
                    P.hop_fin = 0.0
                    try:
                        y = next(ent[2])
                    except StopIteration:
                        live.pop(0)
                        continue
                    if P.hop_fin > 0.0:
                        ent[0] = P.hop_fin
                    if isinstance(y, tuple) and y[0] == "spawn":
                        for g2 in y[1]:
                            live.append([ent[0], cnt, g2])
                            cnt += 1

            def S34pair(R):
                for t_ in range(TT):
                    yield from S34(R * TT + t_)

            run_round([(S1(sb_lo), 1)])
            if deep:
                for R in range(sb_lo, sb_hi + 1):
                    gens = []
                    if R - 1 >= sb_lo:
                        gens.append((S34pair(R - 1), 1))
                    if R < sb_hi:
                        gens += [(S2(R * TT + t_), 1) for t_ in range(TT)]
                    if R + 1 < sb_hi:
                        gens.append((S1(R + 1), 1))
                    run_round(gens)
            else:
                for r_ in range(sb_lo * TT, sb_hi * TT + 1):
                    gens = []
                    if r_ >= sb_lo * TT + 1:
                        gens.append((S34(r_ - 1), 1))
                    if r_ < sb_hi * TT:
                        gens.append((S2(r_), 1))
                    if r_ % TT == TT - 1 and (r_ + 1) // TT < sb_hi:
                        gens.append((S1((r_ + 1) // TT), 1))
                    run_round(gens)
            if os.environ.get("KDEBUG_SBUF"):
                print("MODEL phase1 end us", {k: round(v, 1) for k, v in P.efree.items()}, flush=True)
            barrier()

        if "nomlp" not in debug:
            with ExitStack() as s2:
                W1 = SB(s2, "W1", [128, 8, 4 * D], BF16)
                W2 = SB(s2, "W2", [128, 32, D], BF16)
                stg2 = [SB(s2, f"fstg{i}", [128, 2048]) for i in range(2)]
                hh2 = [[SB(s2, f"hh{c}_{i}", [128, D]) for i in range(TT)] for c in range(2)]
                nb = [SB(s2, f"nb{i}", [128, D], BF16) for i in range(TT)]
                nT2 = [SB(s2, f"nT{c}", [128, 8, SBT], BF16) for c in range(2)]
                aT = SB(s2, "aT", [128, 32, SBT], BF16)
                rz = [SB(s2, f"rz{i}", [128, SBT]) for i in range(2)]
                obuf = [SB(s2, f"obuf{i}", [128, D]) for i in range(2)]
                junk2 = SB(s2, "junk2", [128, D], BF16)
                fss = [SB(s2, f"fss{i}", [128, 4]) for i in range(2)]
                engs = ["dve", "pool"]
                ci = 0
                for kt in range(8):
                    for hf in range(2):
                        st_ = stg2[ci % 2]
                        dma(st_[:], w_ff1[kt * 128:(kt + 1) * 128, hf * 2048:(hf + 1) * 2048])
                        scale_cast(ci, W1[:, kt, hf * 2048:(hf + 1) * 2048], st_[:], fg[:, kt:kt + 1])
                        ci += 1
                pcnt2 = [0]

                def npz():
                    pcnt2[0] += 1
                    return psb[pcnt2[0] % 4]
                def prologue(ob):
                    hh_, nT = hh2[ob % 2], nT2[ob % 2]
                    for tt in range(TT):
                        row0 = ob * SBT + tt * 128
                        fs = fss[tt % 2]
                        dma(hh_[tt][:], out_d[row0:row0 + 128, :], rk=("dout", row0))
                        act(junk2[:], hh_[tt][:], AF.Square, accum=fs[:, 0:1])
                        rstd_(fs[:, 2:3], fs[:, 0:1], 1.0 / D, EPS)
                        scale_cast(tt, nb[tt][:], hh_[tt][:], fs[:, 2:3])
                    for kt in range(8):
                        for tt in range(TT):
                            tr(psT[:, tt * 128:(tt + 1) * 128], nb[tt][:, kt * 128:(kt + 1) * 128], ident_bf[:])
                        cp("act" if kt % 2 else "dve", nT[:, kt, :], psT[:, 0:SBT])

                prologue(0)
                for f2 in range(16):
                    st_ = stg2[ci % 2]
                    dma(st_[:].rearrange("p (a n) -> p a n", a=2), w_ff2[f2 * 256:(f2 + 1) * 256, :].rearrange("(a p) n -> p a n", p=128))
                    cp(["dve", "act"][ci % 2], W2[:, 2 * f2:2 * f2 + 2, :], st_[:].rearrange("p (a n) -> p a n", a=2))
                    ci += 1
                for ob in range(nown):
                    hh_, nT = hh2[ob % 2], nT2[ob % 2]
                    for ft in range(32):
                        pz = npz()
                        for kt in range(8):
                            mm(pz[:, 0:SBT], W1[:, kt, ft * 128:(ft + 1) * 128], nT[:, kt, :], start=(kt == 0), stop=(kt == 7))
                        rz_ = rz[ft % 2]
                        act(rz_[:], pz[:, 0:SBT], AF.Relu)
                        tten("pool", aT[:, ft, :], rz_[:], rz_[:], ALU.mult)
                    if ob + 1 < nown:
                        prologue(ob + 1)
                    for tt in range(TT):
                        row0 = ob * SBT + tt * 128
                        ob_ = obuf[tt % 2]
                        for half in range(2):
                            po = psb[4 + half]
                            for ft in range(32):
                                mm(po[:, 0:512], aT[:, ft, tt * 128:(tt + 1) * 128], W2[:, ft, half * 512:(half + 1) * 512],
                                   start=(ft == 0), stop=(ft == 31))
                            tten("dve", ob_[:, half * 512:(half + 1) * 512], po[:, 0:512], hh_[tt][:, half * 512:(half + 1) * 512], ALU.add)
                        dma(out_d[row0:row0 + 128, :], ob_[:], wk=("dout", row0))
        P.finish()
        P.emit(nc)
    return nc, dbg_out


def _consts():
    i = np.arange(128)
    ident = np.eye(128, dtype=np.float32)
    ones = np.ones((128, 128), np.float32)
    tri = (i[:, None] <= i[None, :]).astype(np.float32)
    blk = ((i[:, None] // 64) == (i[None, :] // 64)).astype(np.float32)
    maskP = np.where(i[None, :] < i[:, None], 0.0, BIG).astype(np.float32)
    maskN = np.where(i[None, :] >= i[:, None], 0.0, -BIG).astype(np.float32)
    return np.ascontiguousarray(np.stack([ident, ones, tri, blk, maskP, maskN], axis=1))


def _bias_table(rel_bias):
    ext = np.concatenate([rel_bias, np.full((8, 1), -BIG, np.float32)], axis=1)
    p = np.arange(128)[:, None, None]
    r = np.arange(5)[None, :, None]
    q = np.arange(128)[None, None, :]
    k = 128 * r + p
    rel = np.clip(512 + q - k, -256, 256) + 256
    kc, qc = k // 64, q // 64
    valid = (kc >= qc) & (kc <= qc + 8)
    idx = np.where(valid, rel, 513)
    tab = ext[:, idx]
    tab = np.transpose(tab, (1, 2, 0, 3))
    return np.ascontiguousarray(tab.reshape(128, 5 * 8 * 128)).astype(np.float32)


def _small(inp, j):
    s = np.zeros((128, 96), np.float32)
    s[:, 0:8] = inp["mix_norm_gain"][0].reshape(8, 128).T
    s[:, 8:12] = inp["att_out_gain"][0].reshape(4, 128).T
    s[:, 12:16] = np.repeat(inp["dn_out_gain"][0][:, None], 4, axis=1)
    s[:, 16:24] = inp["ffn_norm_gain"][0].reshape(8, 128).T
    cwt = inp["dn_conv_w"][0]
    s[:, 24:72] = np.transpose(cwt.reshape(4, 12, 128), (2, 1, 0)).reshape(128, 48)
    s[:, 72] = np.tile(inp["att_q_gain"][0], 2)
    s[:, 73] = np.tile(inp["att_k_gain"][0], 2)
    s[:, 74:78] = np.broadcast_to(inp["dn_dt_bias"][0][None, :], (128, 4))
    s[:, 78:82] = np.broadcast_to(inp["dn_a_log"][0][None, :], (128, 4))
    if j == 0:
        s[:, 82] = np.float32(-BIG)
    return s


_CACHE = {}


def kernel(**inputs):
    inp = {k: np.asarray(v, dtype=np.float32) for k, v in inputs.items()}
    x = inp["x"]
    B, L, _ = x.shape
    nq = 4
    own_tok = L // nq
    nown = own_tok // SBT
    npre = (L - own_tok) // SBT
    key = (npre, nown)
    if key not in _CACHE:
        _CACHE[key] = build_program(npre, nown)[0]
    nc = _CACHE[key]
    cst = _consts()
    bias = _bias_table(inp["rel_bias"][0])
    in_maps = []
    for c in range(8):
        b, j = c // nq, c % nq
        xsn = np.zeros(((npre + nown) * SBT, D), np.float32)
        n_real = own_tok * (j + 1)
        xsn[xsn.shape[0] - n_real:, :] = x[b, :n_real, :]
        in_maps.append({
            "xs": xsn, "w_in": inp["w_in"][0], "w_out": inp["w_out"][0], "w_ff1": inp["w_ff1"][0],
            "w_ff2": inp["w_ff2"][0], "cst": cst, "biasT": bias, "small": _small(inp, j),
        })
    res = run_bass_kernel_spmd(nc, in_maps, core_ids=list(range(8)))
    out = np.zeros((B, L, D), np.float32)
    for c in range(8):
        b, j = c // nq, c % nq
        out[b, j * own_tok:(j + 1) * own_tok, :] = res.results[c]["out"]
    return out
```

```python
import os
from contextlib import ExitStack

import numpy as np
import concourse.bass as bass
import concourse.mybir as mybir
from concourse.bass_utils import run_bass_kernel_spmd

F32 = mybir.dt.float32
BF16 = mybir.dt.bfloat16
AF = mybir.ActivationFunctionType
ALU = mybir.AluOpType

D = 1024
SBT = 256
TT = SBT // 128
EPS = 1e-6
BIG = 30000.0
INC = 3592


class Prog:
    ENG = ("pe", "act", "dve", "pool", "sp")
    M = 30000
    K = 8

    def __init__(self):
        self.ops = {e: [] for e in self.ENG}
        self.ndma = {e: 0 for e in self.ENG}
        self.waited = {e: {} for e in self.ENG}
        self.state = {}
        self.efree = {e: 0.0 for e in self.ENG}
        self.tfin = {}
        self.hop_fin = 0.0

    @staticmethod
    def _nm(a):
        if isinstance(a, str):
            return a
        t = getattr(a, "tensor", None)
        return t.name if t is not None else a.name

    @classmethod
    def _key(cls, x):
        if isinstance(x, tuple):
            return (cls._nm(x[0]), x[1])
        return (cls._nm(x), None)

    def _st(self, name):
        st = self.state.get(name)
        if st is None:
            st = {"w": {}, "r": {}}
            self.state[name] = st
        return st

    def op(self, eng, fn, r=(), w=(), dma=False, dur=0.3):
        r2, w2 = [], []
        for x in r:
            if x is None or isinstance(x, (int, float)):
                continue
            k = self._key(x)
            if k[0].startswith("ps"):
                w2.append((k[0], None))
            else:
                r2.append(k)
        for x in w:
            k = self._key(x)
            w2.append((k[0], None) if k[0].startswith("ps") else k)
        r, w = r2, w2
        deps = set()
        idx = len(self.ops[eng])
        if dma:
            n = self.ndma[eng]
            self.ndma[eng] += 1
            tok = ("d", eng, n)
            if n >= self.K:
                deps.add(("d", eng, n - self.K))
        else:
            tok = ("c", eng, idx)
        for (name, sub) in r:
            st = self._st(name)
            subs = [sub, None] if sub is not None else list(st["w"].keys())
            for s in subs:
                t = st["w"].get(s)
                if t is not None:
                    deps.add(t)
        for (name, sub) in w:
            st = self._st(name)
            subs = [sub, None] if sub is not None else list(set(st["w"].keys()) | set(st["r"].keys()))
            for s in subs:
                t = st["w"].get(s)
                if t is not None:
                    deps.add(t)
                for t in st["r"].get(s, {}).values():
                    deps.add(t)
        for (name, sub) in r:
            st = self._st(name)
            rk = (tok[0], tok[1]) if tok[0] == "c" else (tok[0], tok[1], tok[2] % self.K)
            st["r"].setdefault(sub, {})[rk] = tok
        for (name, sub) in w:
            st = self._st(name)
            if sub is None:
                st["w"] = {None: tok}
                st["r"] = {}
            else:
                st["w"][sub] = tok
                st["r"][sub] = {}
        t_start = self.efree[eng]
        for t in deps:
            if t != tok:
                t_start = max(t_start, self.tfin.get(t, 0.0) + (0.1 if (t[0] == "c" and t[1] == eng) else 0.35))
        if dma:
            self.efree[eng] = t_start + 0.1
            t_end = t_start + 2.0 + dur
        else:
            t_end = t_start + dur
            self.efree[eng] = t_end
        self.tfin[tok] = t_end
        self.hop_fin = max(self.hop_fin, t_end)
        waits = {}
        for t in deps:
            if t == tok:
                continue
            if t[0] == "c":
                if t[1] == eng and eng == "pe" and not dma:
                    continue
                wk = ("c", t[1])
            else:
                wk = ("d", t[1], t[2] % self.K)
            if self.waited[eng].get(wk, -1) >= t[2]:
                continue
            if wk not in waits or waits[wk][2] < t[2]:
                waits[wk] = t
        for wk, t in waits.items():
            self.waited[eng][wk] = t[2]
            if t[0] == "c":
                self.ops[t[1]][t[2]]["needed"] = True
        self.ops[eng].append({"fn": fn, "waits": list(waits.values()), "needed": False, "dma": dma, "tok": tok})
        return tok

    def dma_tail_waits(self):
        waits = []
        for e in self.ENG:
            n = self.ndma[e]
            for slot in range(min(n, self.K)):
                last = ((n - 1 - slot) // self.K) * self.K + slot
                waits.append(("d", e, last))
        return waits

    def finish(self):
        self.ops["sp"].append({"fn": None, "waits": self.dma_tail_waits(), "needed": False, "dma": False, "tok": None})

    def emit(self, nc):
        semnames = []
        seen = set()

        def sem(name):
            if name not in seen:
                seen.add(name)
                semnames.append(name)
            return name

        for e in self.ENG:
            c = 0
            for o in self.ops[e]:
                if o["dma"]:
                    n = o["tok"][2]
                    o["inc"] = (sem(f"d{e}{n % self.K}"), 16)
                elif o["needed"]:
                    c += 1
                    k = (c - 1) // self.M
                    o["inc"] = (sem(f"{e}{k}"), 1)
                    o["val"] = c - k * self.M
                else:
                    o["inc"] = None

        def resolve(t):
            if t[0] == "d":
                return (f"d{t[1]}{t[2] % self.K}", 16 * (t[2] // self.K + 1))
            o = self.ops[t[1]][t[2]]
            return (o["inc"][0], o["val"])

        with ExitStack() as stack:
            sems = {name: stack.enter_context(nc.semaphore(name)) for name in semnames}
            with nc.Block() as block:
                def mk(eng):
                    def body(e):
                        for o in self.ops[eng]:
                            for t in o["waits"]:
                                s, v = resolve(t)
                                e.wait_ge(sems[s], v)
                            if o["fn"] is not None:
                                ins = o["fn"](e)
                                if o["inc"] is not None:
                                    ins.then_inc(sems[o["inc"][0]], o["inc"][1])
                    return body
                block.tensor(mk("pe"))
                block.scalar(mk("act"))
                block.vector(mk("dve"))
                block.gpsimd(mk("pool"))
                block.sync(mk("sp"))


def build_program(npre, nown, debug=None):
    NS = npre + nown
    NTOK = NS * SBT
    debug = debug or ()
    nc = bass.Bass("TRN2", target_bir_lowering=False)
    dr = lambda name, shape, kind="ExternalInput": nc.dram_tensor(name, shape, F32, kind=kind).ap()
    xs = dr("xs", [NTOK, D])
    w_in = dr("w_in", [D, INC])
    w_out = dr("w_out", [D, D])
    w_ff1 = dr("w_ff1", [D, 4 * D])
    w_ff2 = dr("w_ff2", [4 * D, D])
    cst = dr("cst", [128, 6, 128])
    bias_d = dr("biasT", [128, 5 * 8 * 128])
    small = dr("small", [128, 96])
    out_d = dr("out", [nown * SBT, D], kind="ExternalOutput")
    dbg_out = {}

    P = Prog()
    root = ExitStack()
    with root:
        sb_bytes = {}
        sfx = [""]

        def SB(stack, name, shape, dt=F32):
            name = name + sfx[0]
            n = 1
            for d_ in shape[1:]:
                n *= d_
            sb_bytes[name] = n * (2 if dt == BF16 else 4)
            if os.environ.get("KDEBUG_SBUF"):
                print("SBUF", name, sb_bytes[name], flush=True)
            return stack.enter_context(nc.sbuf_tensor(name, shape, dt))

        def PS(stack, name, shape, dt=F32):
            return stack.enter_context(nc.psum_tensor(name, shape, dt))

        def fsz(ap):
            n = 1
            for d_ in ap.shape[1:]:
                n *= d_
            return n

        def mm(out, lhsT, rhs, start=True, stop=True, skip=False):
            n = fsz(rhs)
            d_ = 0.12 + n * 0.0003
            if lhsT.dtype == F32:
                d_ *= 2.0
            P.op("pe", lambda e: e.matmul(out, lhsT=lhsT, rhs=rhs, start=start, stop=stop, skip_group_check=skip),
                 r=[lhsT, rhs], w=[out], dur=d_)

        def tr(out, in_, ident):
            P.op("pe", lambda e: e.transpose(out=out, in_=in_, identity=ident), r=[in_, ident], w=[out],
                 dur=0.3 if in_.dtype == F32 else 0.16)

        def act(out, in_, func, bias=0.0, scale=1.0, accum=None):
            d_ = 0.2 + fsz(out) * 0.00095
            if accum is None:
                P.op("act", lambda e: e.activation(out=out, in_=in_, func=func, bias=bias, scale=scale),
                     r=[in_, bias, scale], w=[out], dur=d_)
            else:
                P.op("act", lambda e: e.activation(out=out, in_=in_, func=func, bias=bias, scale=scale, accum_out=accum),
                     r=[in_, bias, scale], w=[out, accum], dur=d_ + 0.1)

        def edur(eng, out):
            return (0.1 + fsz(out) * 0.0023) if eng == "pool" else (0.1 + fsz(out) * 0.0012)

        def tten(eng, out, in0, in1, op):
            P.op(eng, lambda e: e.tensor_tensor(out=out, in0=in0, in1=in1, op=op), r=[in0, in1], w=[out], dur=edur(eng, out))

        def ts(eng, out, in0, s1, s2, op0, op1=None):
            if op1 is None:
                P.op(eng, lambda e: e.tensor_scalar(out=out, in0=in0, scalar1=s1, scalar2=None, op0=op0), r=[in0, s1], w=[out],
                     dur=edur(eng, out))
            else:
                P.op(eng, lambda e: e.tensor_scalar(out=out, in0=in0, scalar1=s1, scalar2=s2, op0=op0, op1=op1),
                     r=[in0, s1, s2], w=[out], dur=edur(eng, out))

        def stt(out, in0, scalar, in1, op0, op1):
            P.op("dve", lambda e: e.scalar_tensor_tensor(out=out, in0=in0, scalar=scalar, in1=in1, op0=op0, op1=op1),
                 r=[in0, scalar, in1], w=[out], dur=0.2 + fsz(out) * 0.0012)

        def cp(eng, out, in_):
            if eng == "act":
                act(out, in_, AF.Copy)
            else:
                P.op(eng, lambda e: e.tensor_copy(out=out, in_=in_), r=[in_], w=[out], dur=edur(eng, out))

        def recip(out, in_):
            P.op("dve", lambda e: e.reciprocal(out=out, in_=in_), r=[in_], w=[out], dur=0.1 + fsz(out) * 0.008)

        def rstd_(out, in_, scale, bias):
            act(out, in_, AF.Ln, bias=bias, scale=scale)
            act(out, out, AF.Exp, scale=-0.5)

        def silu_(out, in_, tmp):
            act(tmp, in_, AF.Exp, scale=-1.0)
            act(tmp, tmp, AF.Ln, bias=1.0)
            act(tmp, tmp, AF.Exp, scale=-1.0)
            tten("dve" if in_.tensor.name.startswith("ps") else "pool", out, in_, tmp, ALU.mult)

        def scale_cast(i, out, in_, sc):
            if i % 2 == 0:
                ts("dve", out, in_, sc, None, ALU.mult)
            else:
                act(out, in_, AF.Copy, scale=sc)

        def memset(eng, ap, val):
            P.op(eng, lambda e: e.memset(ap, val), w=[ap])

        def dma(out, in_, rk=None, wk=None):
            P.op("sp", lambda e: e.dma_start(out=out, in_=in_), r=[rk if rk is not None else in_],
                 w=[wk if wk is not None else out], dma=True, dur=fsz(out) * 0.0004)

        def dump(name, src, shape):
            if name not in debug:
                return
            d = nc.dram_tensor("dbg_" + name, shape, F32, kind="ExternalOutput").ap()
            dbg_out[name] = d
            dma(d, src)

        ident_f = SB(root, "ident_f", [128, 128])
        ones_f = SB(root, "ones_f", [128, 128])
        tri_f = SB(root, "tri_f", [128, 128])
        maskP = SB(root, "maskP", [128, 128])
        maskN = SB(root, "maskN", [128, 128])
        ident_bf = SB(root, "ident_bf", [128, 128], BF16)
        ones_bf = SB(root, "ones_bf", [128, 128], BF16)
        blk_bf = SB(root, "blk_bf", [128, 128], BF16)
        smallt = SB(root, "smallt", [128, 96])
        bar_a = SB(root, "bar_a", [128, 4])
        bar_s = SB(root, "bar_s", [128, 8])
        psb = [PS(root, f"ps{i}", [128, 512]) for i in range(7)]
        psT = PS(root, "psT", [128, 1024], BF16)

        def barrier():
            P.ops["sp"].append({"fn": None, "waits": P.dma_tail_waits(), "needed": False, "dma": False, "tok": None})
            P.op("sp", lambda e: e.dma_start(out=bar_s[:, 4:8], in_=bar_s[:, 0:4]), r=[bar_s], w=[("BAR", "sp"), bar_s], dma=True)
            P.op("pe", lambda e: e.matmul(psb[6][0:1, 0:1], lhsT=ident_bf[:, 0:1], rhs=ident_bf[:, 0:1], start=True, stop=True),
                 r=[ident_bf], w=[("BAR", "pe"), psb[6]])
            P.op("act", lambda e: e.activation(out=bar_a[:, 0:1], in_=ident_f[:, 0:1], func=AF.Copy), r=[ident_f], w=[("BAR", "act")])
            P.op("dve", lambda e: e.memset(bar_a[:, 1:2], 0.0), w=[("BAR", "dve")])
            P.op("pool", lambda e: e.memset(bar_a[:, 2:3], 0.0), w=[("BAR", "pool")])
            P.op("pe", lambda e: e.matmul(psb[6][0:1, 0:1], lhsT=ident_bf[:, 0:1], rhs=ident_bf[:, 0:1], start=True, stop=True),
                 r=["BAR", ident_bf], w=[psb[6]])
            P.op("act", lambda e: e.activation(out=bar_a[:, 0:1], in_=ident_f[:, 0:1], func=AF.Copy), r=["BAR", ident_f])
            P.op("dve", lambda e: e.memset(bar_a[:, 1:2], 0.0), r=["BAR"])
            P.op("pool", lambda e: e.memset(bar_a[:, 2:3], 0.0), r=["BAR"])
            P.op("sp", lambda e: e.dma_start(out=bar_s[:, 4:8], in_=bar_s[:, 0:4]), r=["BAR", bar_s], w=[("BAR2", "sp"), bar_s], dma=True)

        with ExitStack() as s0:
            cst_f = SB(s0, "cst_f", [128, 6, 128])
            dma(cst_f[:], cst)
            dma(smallt[:], small)
            cp("dve", ident_f[:], cst_f[:, 0, :])
            cp("dve", ones_f[:], cst_f[:, 1, :])
            cp("dve", tri_f[:], cst_f[:, 2, :])
            cp("dve", maskP[:], cst_f[:, 4, :])
            cp("dve", maskN[:], cst_f[:, 5, :])
            cp("pool", ident_bf[:], cst_f[:, 0, :])
            cp("pool", ones_bf[:], cst_f[:, 1, :])
            cp("pool", blk_bf[:], cst_f[:, 3, :])
            memset("pool", bar_s[:], 0.0)
            barrier()
        mg = smallt[:, 0:8]
        wog = smallt[:, 8:16]
        fg = smallt[:, 16:24]
        cw = smallt[:, 24:72]
        qg = smallt[:, 72:73]
        kg = smallt[:, 73:74]
        dtb = smallt[:, 74:78]
        alog = smallt[:, 78:82]
        hmask = smallt[:, 82:83]
        nexpA = smallt[:, 84:88]
        qg8 = smallt[:, 88:89]

        halo = SB(root, "halo", [128, 12, 3])
        Sst = SB(root, "Sst", [128, 4, 128])
        Sbf = SB(root, "Sbf", [128, 4, 128], BF16)
        memset("pool", halo[:], 0.0)
        memset("pool", Sst[:], 0.0)
        memset("pool", Sbf[:], 0.0)
        act(nexpA, alog, AF.Exp)
        ts("dve", nexpA, nexpA, -1.0, None, ALU.mult)
        ts("dve", qg8, qg, 0.125, None, ALU.mult)
        phases = [(0, npre - 2, True), (npre - 2, NS, False)] if npre - 2 >= 2 else [(0, NS, False)]
        for ph, (sb_lo, sb_hi, deep) in enumerate(phases):
          with ExitStack() as s1:
            sfx[0] = f"_{ph}"
            NA = 3 if deep else 2
            NBS = 4 if deep else 2
            NS2 = 2 if deep else 1
            if deep:
                Win = SB(s1, "Win", [128, 8, 1544], BF16)

                def Wc(kt, c0, c1, Win=Win):
                    o = 1536 if c0 < 3584 else 3584 - 1536
                    return Win[:, kt, c0 - o:c1 - o]
            else:
                Win = SB(s1, "Win", [128, 8, INC], BF16)
                Wout = SB(s1, "Wout", [128, 8, D], BF16)
                biasT = SB(s1, "biasTb", [128, 5, 8, 128], BF16)

                def Wc(kt, c0, c1, Win=Win):
                    return Win[:, kt, c0:c1]
            with ExitStack() as s1a:
                stg = [SB(s1a, f"stg{i}", [128, INC]) for i in range(2)]
                for kt in range(8):
                    st_ = stg[kt % 2]
                    if deep:
                        dma(st_[:, 0:1536], w_in[kt * 128:(kt + 1) * 128, 1536:3072])
                        dma(st_[:, 1536:1544], w_in[kt * 128:(kt + 1) * 128, 3584:3592])
                        scale_cast(kt, Win[:, kt, :], st_[:, 0:1544], mg[:, kt:kt + 1])
                    else:
                        dma(st_[:], w_in[kt * 128:(kt + 1) * 128, :])
                        scale_cast(kt, Win[:, kt, :], st_[:], mg[:, kt:kt + 1])
                if not deep:
                    for kt in range(8):
                        st_ = stg[kt % 2]
                        dma(st_[:, 0:D], w_out[kt * 128:(kt + 1) * 128, :])
                        scale_cast(kt, Wout[:, kt, :], st_[:, 0:D], wog[:, kt:kt + 1])
                    for r_ in range(5):
                        st_ = stg[r_ % 2]
                        dma(st_[:, 0:1024], bias_d[:, r_ * 1024:(r_ + 1) * 1024])
                        cp(["dve", "act"][r_ % 2], biasT[:, r_, :, :], st_[:, 0:1024].rearrange("p (h q) -> p h q", h=8))
                barrier()

            xt = [SB(s1, f"xt{i}", [128, D]) for i in range(3 if deep else 1)]
            nss = [SB(s1, f"nss{i}", [128, 4]) for i in range(3)]
            ub = [SB(s1, f"ub{i}", [128, D], BF16) for i in range(TT)]
            prw = [SB(s1, f"prw{i}", [128, SBT + 3]) for i in range(3)]
            cv = [SB(s1, f"cv{i}", [128, SBT]) for i in range(3)]
            cs = [SB(s1, f"cs{i}", [128, SBT]) for i in range(3)]
            sq = [SB(s1, f"sq{i}", [128, SBT], BF16) for i in range(4)]
            sd2 = [None, None, None, SB(s1, "sd23", [128, SBT])]
            kraw = [SB(s1, "kraw0", [128, SBT])]
            uT = [SB(s1, f"uT{a}", [128, 8, SBT], BF16) for a in range(NA)]
            dqT = [[SB(s1, f"dqT{a}_{h}", [128, SBT], BF16) for h in range(4)] for a in range(NA)]
            dkT = [[SB(s1, f"dkT{a}_{h}", [128, SBT], BF16) for h in range(4)] for a in range(NA)]
            dvT = [SB(s1, f"dvT{h}", [128, SBT], BF16) for h in range(4)]
            Ktm = [SB(s1, f"Ktm{a}", [128, 4, TT, 128], BF16) for a in range(NA)]
            Vtm = [SB(s1, f"Vtm{a}", [128, 4, TT, 128], BF16) for a in range(NA)]
            ab = [SB(s1, f"ab{i}", [128, 8]) for i in range(TT)]
            gsm = [[SB(s1, f"gsm{a}_{i}", [128, 24]) for i in range(TT)] for a in range(NA)]
            Gb_ = [SB(s1, f"Gb{k}", [128, 4, 128]) for k in range(NS2)]
            Lm_ = [SB(s1, f"Lm{k}", [128, 4, 128]) for k in range(NS2)]
            LTm_ = Gb_
            At_ = [[SB(s1, f"At{k}_{i}", [128, 4, 128], BF16) for i in range(2)] for k in range(NS2)]
            Bt_ = [[SB(s1, f"Bt{k}_{i}", [128, 4, 128], BF16) for i in range(2)] for k in range(NS2)]
            XB = [SB(s1, f"XB{i}", [128, 4, 128], BF16) for i in range(NBS)]
            TTb = [SB(s1, f"TTb{i}", [128, 4, 128], BF16) for i in range(NBS)]
            PTd = [SB(s1, f"PTd{i}", [128, 4, 128], BF16) for i in range(NBS)]
            Vb = [SB(s1, f"Vb{i}", [128, 4, 128]) for i in range(NBS)]
            Kd = [SB(s1, f"Kd{i}", [128, 4, 128], BF16) for i in range(NBS)]
            dsm = [SB(s1, f"dsm{i}", [128, 32]) for i in range(NBS)]
            Rr = SB(s1, "Rr", [128, 4, 128], BF16)
            Vn0 = SB(s1, "Vn0", [128, 4, 128], BF16)
            rres = SB(s1, "rres", [128, 4, 128], BF16)
            Vn = SB(s1, "Vn", [128, 4, 128], BF16)
            O1 = SB(s1, "O1", [128, 4, 128])
            Odn = SB(s1, "Odn", [128, 4, 128])
            if not deep:
                NR = 4
                akT = [SB(s1, f"akT{i}", [128, 4, SBT], BF16) for i in range(NR)]
                vaug = [SB(s1, f"vaug{i}", [128, TT, 8, 65], BF16) for i in range(NR)]
                aqTz = SB(s1, "aqTz", [128, 4, 2, 128], BF16)
                sg = SB(s1, "sg", [128, 512])
                PTa = [SB(s1, f"PTa{i}", [128, 512], BF16) for i in range(2)]
                oatt = SB(s1, "oatt", [128, 512])
                rec = SB(s1, "rec", [128, 16])
                cat = SB(s1, "cat", [128, D], BF16)
                catT = SB(s1, "catT", [128, D], BF16)
                xres_ = SB(s1, "xres", [128, D])
                qraw_ = SB(s1, "qraw4", [128, 512])
                qsd_ = SB(s1, "qsd4", [128, 512])

            if not deep:
                memset("pool", aqTz[:], 0.0)
                for i in range(NR):
                    memset("pool", vaug[i][:], 1.0)

            pcnt = [0]

            def next_pp():
                pcnt[0] += 1
                return psb[pcnt[0] % 2]
            dcnt = [0]

            pd_sets = [[2, 3], [5, 6]] if deep else [[2, 3]]
            dcnt2 = [0, 0]

            def next_pd_of(k2):
                def f():
                    dcnt2[k2] += 1
                    return psb[pd_sets[k2][dcnt2[k2] % 2]]
                return f
            pS3 = psb[4]

            def S1(sb):
                A = sb % NA
                att = sb >= npre - 2
                for tt in range(TT):
                    x_ = xt[(sb * TT + tt) % len(xt)]
                    ns_ = nss[tt]
                    dma(x_[:], xs[sb * SBT + tt * 128: sb * SBT + (tt + 1) * 128, :])
                    act(ub[tt][:], x_[:], AF.Square, accum=ns_[:, 0:1])
                    rstd_(ns_[:, 2:3], ns_[:, 0:1], 1.0 / D, EPS)
                    scale_cast(tt, ub[tt][:], x_[:], ns_[:, 2:3])
                    yield
                for kt in range(8):
                    for tt in range(TT):
                        tr(psT[:, tt * 128:(tt + 1) * 128], ub[tt][:, kt * 128:(kt + 1) * 128], ident_bf[:])
                    cp("act" if kt % 2 else "dve", uT[A][:, kt, :], psT[:, 0:SBT])
                    if kt % 2:
                        yield
                groups = [S1grp(sb, 1), S1grp(sb, 2), S1ab(sb)]
                if sb >= npre - 1:
                    groups.insert(0, S1grp(sb, 0))
                yield ("spawn", groups)

            def S1grp(sb, grp):
                A = sb % NA
                pw, cv_, cs_, sq_, sd_ = prw[grp], cv[grp], cs[grp], sq[grp], cv[grp]
                for h in range(4):
                    ft = 4 * grp + h
                    pp = next_pp()
                    c0 = 1536 + 128 * ft
                    for kt in range(8):
                        mm(pp[:, 0:SBT], Wc(kt, c0, c0 + 128), uT[A][:, kt, :], start=(kt == 0), stop=(kt == 7))
                    cp("act", pw[:, 3:3 + SBT], pp[:, 0:SBT])
                    P.op("pool", lambda e, pw=pw, ft=ft: e.tensor_copy(out=pw[:, 0:3], in_=halo[:, ft, :]), r=[(halo, ft)], w=[pw])
                    yield
                    ts("dve", cv_[:], pw[:, 0:SBT], cw[:, ft * 4:ft * 4 + 1], None, ALU.mult)
                    for j in range(1, 4):
                        stt(cv_[:], pw[:, j:j + SBT], cw[:, ft * 4 + j:ft * 4 + j + 1], cv_[:], ALU.mult, ALU.add)
                    P.op("pool", lambda e, pw=pw, ft=ft: e.tensor_copy(out=halo[:, ft, :], in_=pw[:, SBT:SBT + 3]), r=[pw], w=[(halo, ft)])
                    silu_(cs_[:], cv_[:], cs_[:])
                    yield
                    if grp < 2:
                        tten("pool", sq_[:], cs_[:], cs_[:], ALU.mult)
                        pn_ = next_pp()
                        mm(pn_[:, 0:SBT], ones_bf[:], sq_[:])
                        sc = 128.0 if grp == 0 else 1.0
                        rstd_(sd_[:], pn_[:, 0:SBT], sc, EPS * sc)
                        tten("dve", (dqT if grp == 0 else dkT)[A][h][:], cs_[:], sd_[:], ALU.mult)
                    else:
                        cp("pool", dvT[h][:], cs_[:])
                    yield
                if grp > 0:
                    srcT, dst = (dkT[A], Ktm[A]) if grp == 1 else (dvT, Vtm[A])
                    for h in range(4):
                        for tt in range(TT):
                            tr(psT[:, (h * TT + tt) * 128:(h * TT + tt + 1) * 128], srcT[h][:, tt * 128:(tt + 1) * 128], ident_bf[:])
                    cp("act", dst[:].rearrange("p h t d -> p (h t d)"), psT[:, 0:4 * TT * 128])
                    yield

            def S1ab(sb):
                A = sb % NA
                att = sb >= npre - 2
                for tt in range(TT):
                    pn_ = next_pp()
                    for kt in range(8):
                        mm(pn_[:, 0:8], uT[A][:, kt, tt * 128:(tt + 1) * 128], Wc(kt, 3584, 3592), start=(kt == 0), stop=(kt == 7))
                    cp("dve", ab[tt][:], pn_[:, 0:8])
                    gs = gsm[A][tt]
                    tten("dve", gs[:, 0:4], ab[tt][:, 0:4], dtb, ALU.add)
                    act(gs[:, 4:8], gs[:, 0:4], AF.Exp)
                    act(gs[:, 8:12], gs[:, 4:8], AF.Ln, bias=1.0)
                    tten("dve", gs[:, 12:16], gs[:, 8:12], nexpA, ALU.mult)
                    act(gs[:, 16:20], ab[tt][:, 4:8], AF.Exp, scale=-1.0)
                    act(gs[:, 16:20], gs[:, 16:20], AF.Ln, bias=1.0)
                    act(gs[:, 16:20], gs[:, 16:20], AF.Exp, scale=-1.0)
                    ts("dve", gs[:, 20:24], gs[:, 16:20], -1.0, None, ALU.mult)
                    yield
                if att:
                    slot = sb % NR
                    kr, sq_, sd_ = kraw[0], sq[3], sd2[3]
                    for p_ in range(4):
                        pp = next_pp()
                        c0 = 512 + 128 * p_
                        for kt in range(8):
                            mm(pp[:, 0:SBT], Wc(kt, c0, c0 + 128), uT[A][:, kt, :], start=(kt == 0), stop=(kt == 7))
                        cp("act", kr[:], pp[:, 0:SBT])
                        tten("pool", sq_[:], kr[:], kr[:], ALU.mult)
                        yield
                        pn_ = next_pp()
                        mm(pn_[:, 0:SBT], blk_bf[:], sq_[:])
                        rstd_(sd_[:], pn_[:, 0:SBT], 1.0 / 64, EPS)
                        stt(akT[slot][:, p_, :], kr[:], kg, sd_[:], ALU.mult, ALU.mult)
                        yield
                    for tt in range(TT):
                        pp = next_pp()
                        for kt in range(8):
                            mm(pp[:, 0:512], uT[A][:, kt, tt * 128:(tt + 1) * 128], Win[:, kt, 1024:1536], start=(kt == 0), stop=(kt == 7))
                        cp("act", vaug[slot][:, tt, :, 0:64], pp[:, 0:512].rearrange("p (h d) -> p h d", h=8))
                        yield

            def S2(blk):
                sb, tt = divmod(blk, TT)
                A = sb % NA
                Bs = blk % NBS
                own = sb >= npre
                tsl = slice(tt * 128, (tt + 1) * 128)
                k2 = blk % NS2
                next_pd = next_pd_of(k2)
                Gb, Lm, LTm, At, Bt = Gb_[k2], Lm_[k2], LTm_[k2], At_[k2], Bt_[k2]
                gs = gsm[A][tt]
                g_ = gs[:, 12:16]
                nbeta = gs[:, 20:24]
                ds_ = dsm[Bs]
                dcol, dlast, ed, nbed, dd, edl, cd = (ds_[:, 4 * i:4 * i + 4] for i in range(7))
                pc = next_pd()
                mm(pc[:, 0:4], tri_f[:], g_)
                mm(pc[:, 4:8], ones_f[:], g_)
                cp("dve", ds_[:, 0:8], pc[:, 0:8])
                tten("dve", Gb[:], ones_f[:].unsqueeze(1).to_broadcast([128, 4, 128]),
                     g_.unsqueeze(2).to_broadcast([128, 4, 128]), ALU.mult)
                yield
                act(ed, dcol, AF.Exp)
                tten("dve", nbed, ed, nbeta, ALU.mult)
                tten("dve", dd, dlast, dcol, ALU.subtract)
                act(edl, dd, AF.Exp)
                act(cd, dlast, AF.Exp)
                pD = next_pd()
                for h in range(4):
                    mm(pD[:, h * 128:(h + 1) * 128], Gb[:, h, :], tri_f[:])
                yield
                for h in range(4):
                    stt(Lm[:, h, :], pD[:, h * 128:(h + 1) * 128], dcol[:, h:h + 1], maskP[:], ALU.subtract, ALU.max)
                if own:
                    for h in range(4):
                        stt(LTm[:, h, :], pD[:, h * 128:(h + 1) * 128], dcol[:, h:h + 1], maskN[:], ALU.subtract, ALU.min)
                act(Lm[:], Lm[:], AF.Exp, scale=-1.0)
                if own:
                    act(LTm[:], LTm[:], AF.Exp)
                pK = next_pd()
                for h in range(4):
                    mm(pK[:, h * 128:(h + 1) * 128], dkT[A][h][:, tsl], dkT[A][h][:, tsl])
                yield
                A0, B0, X = At[0], Bt[0], XB[Bs]
                for h in range(4):
                    stt(A0[:, h, :], pK[:, h * 128:(h + 1) * 128], nbeta[:, h:h + 1], Lm[:, h, :], ALU.mult, ALU.mult)
                if own:
                    pQ = next_pd()
                    for h in range(4):
                        mm(pQ[:, h * 128:(h + 1) * 128], dkT[A][h][:, tsl], dqT[A][h][:, tsl])
                    tten("dve", PTd[Bs][:].rearrange("p h c -> p (h c)"), pQ[:, 0:512], LTm[:].rearrange("p h c -> p (h c)"), ALU.mult)
                yield
                pBt = next_pd()[:].bitcast(BF16)
                for h in range(4):
                    tr(pBt[:, h * 128:(h + 1) * 128], A0[:, h, :], ident_bf[:])
                cp("act", B0[:].rearrange("p h c -> p (h c)"), pBt[:, 0:512])
                tten("pool", X[:], B0[:], ident_f[:].unsqueeze(1).to_broadcast([128, 4, 128]), ALU.add)
                tten("pool", TTb[Bs][:], ident_f[:].unsqueeze(1).to_broadcast([128, 4, 128]), B0[:], ALU.subtract)
                yield
                Ap, Bp = A0, B0
                for lvl in range(1, 6):
                    An, Bn = At[lvl % 2], Bt[lvl % 2]
                    pA = next_pd()
                    for h in range(4):
                        mm(pA[:, h * 128:(h + 1) * 128], Bp[:, h, :], Ap[:, h, :])
                    cp("act", An[:].rearrange("p h c -> p (h c)"), pA[:, 0:512])
                    if lvl < 5:
                        pB = next_pd()
                        for h in range(4):
                            mm(pB[:, h * 128:(h + 1) * 128], Ap[:, h, :], Bp[:, h, :])
                        cp("dve", Bn[:].rearrange("p h c -> p (h c)"), pB[:, 0:512])
                    yield
                    pX = next_pd()
                    for h in range(4):
                        mm(pX[:, h * 128:(h + 1) * 128], An[:, h, :], X[:, h, :])
                    tten("dve", X[:].rearrange("p h c -> p (h c)"), pX[:, 0:512], X[:].rearrange("p h c -> p (h c)"), ALU.add)
                    yield
                    Ap, Bp = An, Bn
                tten("pool", Vb[Bs][:], Vtm[A][:, :, tt, :], gs[:, 16:20].unsqueeze(2).to_broadcast([128, 4, 128]), ALU.mult)
                tten("pool", Kd[Bs][:], Ktm[A][:, :, tt, :], edl.unsqueeze(2).to_broadcast([128, 4, 128]), ALU.mult)
                yield

            def S3a(blk):
                sb, tt = divmod(blk, TT)
                A = sb % NA
                Bs = blk % NBS
                own = sb >= npre
                tsl = slice(tt * 128, (tt + 1) * 128)
                ds_ = dsm[Bs]
                dcol, dlast, ed, nbed, dd, edl, cd = (ds_[:, 4 * i:4 * i + 4] for i in range(7))
                X = XB[Bs]
                for h in range(4):
                    mm(pS3[:, h * 128:(h + 1) * 128], dkT[A][h][:, tsl], Sbf[:, h, :])
                for h in range(4):
                    stt(Rr[:, h, :], pS3[:, h * 128:(h + 1) * 128], nbed[:, h:h + 1], Vb[Bs][:, h, :], ALU.mult, ALU.add)
                yield
                for h in range(4):
                    mm(pS3[:, h * 128:(h + 1) * 128], X[:, h, :], Rr[:, h, :])
                cp("act", Vn0[:].rearrange("p h c -> p (h c)"), pS3[:, 0:512])
                yield
                for h in range(4):
                    mm(pS3[:, h * 128:(h + 1) * 128], TTb[Bs][:, h, :], Vn0[:, h, :])
                stt(rres[:].rearrange("p h c -> p (h c)"), pS3[:, 0:512], -1.0, Rr[:].rearrange("p h c -> p (h c)"), ALU.mult, ALU.add)
                yield
                for h in range(4):
                    mm(pS3[:, h * 128:(h + 1) * 128], X[:, h, :], rres[:, h, :])
                tten("dve", Vn[:].rearrange("p h c -> p (h c)"), pS3[:, 0:512], Vn0[:].rearrange("p h c -> p (h c)"), ALU.add)
                yield
                if own:
                    for h in range(4):
                        mm(pS3[:, h * 128:(h + 1) * 128], dqT[A][h][:, tsl], Sbf[:, h, :])
                    for h in range(4):
                        ts("dve", O1[:, h, :], pS3[:, h * 128:(h + 1) * 128], ed[:, h:h + 1], None, ALU.mult)
                    yield
                    for h in range(4):
                        mm(pS3[:, h * 128:(h + 1) * 128], PTd[Bs][:, h, :], Vn[:, h, :])
                    tten("dve", Odn[:].rearrange("p h c -> p (h c)"), pS3[:, 0:512], O1[:].rearrange("p h c -> p (h c)"), ALU.add)
                    yield
                for h in range(4):
                    mm(pS3[:, h * 128:(h + 1) * 128], Kd[Bs][:, h, :], Vn[:, h, :])
                for h in range(4):
                    stt(Sst[:, h, :], Sst[:, h, :], cd[:, h:h + 1], pS3[:, h * 128:(h + 1) * 128], ALU.mult, ALU.add)
                cp("act", Sbf[:], Sst[:])
                yield
                if not own:
                    return
                ob = sb - npre
                row0 = ob * SBT + tt * 128
                T0 = sb * SBT + tt * 128
                if "odn" in debug:
                    dmp = dbg_out.get("odn")
                    if dmp is None:
                        dmp = nc.dram_tensor("dbg_odn", [nown * SBT, 512], F32, kind="ExternalOutput").ap()
                        dbg_out["odn"] = dmp
                    dma(dmp[row0:row0 + 128, :], Odn[:].rearrange("p h c -> p (h c)"), wk=("dbgodn", row0))
                for h in range(4):
                    act(cat[:, 512 + h * 128:512 + (h + 1) * 128], Odn[:, h, :], AF.Square, accum=rec[:, h:h + 1])
                rstd_(rec[:, 8:12], rec[:, 0:4], 1.0 / 128, EPS)
                pg = psb[4]
                for kt in range(8):
                    mm(pg[:, 0:512], uT[A][:, kt, tsl], Win[:, kt, 3072:3584], start=(kt == 0), stop=(kt == 7))
                silu_(sg[:], pg[:, 0:512], sg[:])
                for h in range(4):
                    stt(cat[:, 512 + h * 128:512 + (h + 1) * 128], Odn[:, h, :], rec[:, 8 + h:9 + h], sg[:, h * 128:(h + 1) * 128],
                        ALU.mult, ALU.mult)
                yield

            def S4att(blk):
                sb, tt = divmod(blk, TT)
                A = sb % NA
                tsl = slice(tt * 128, (tt + 1) * 128)
                ob = sb - npre
                row0 = ob * SBT + tt * 128
                T0 = sb * SBT + tt * 128
                pq, pq2 = psb[5], psb[6]
                qraw4 = qraw_[:]
                qsd4 = qsd_[:]
                qsq4 = cat[:, 0:512]
                for p_ in range(4):
                    c0 = 128 * p_
                    for kt in range(8):
                        mm(pq[:, p_ * 128:(p_ + 1) * 128], Win[:, kt, c0:c0 + 128], uT[A][:, kt, tsl], start=(kt == 0), stop=(kt == 7))
                cp("act", qraw4, pq[:, 0:512])
                tten("pool", qsq4, qraw4, qraw4, ALU.mult)
                yield
                for p_ in range(4):
                    mm(pq2[:, p_ * 128:(p_ + 1) * 128], blk_bf[:], qsq4[:, p_ * 128:(p_ + 1) * 128])
                rstd_(qsd4, pq2[:, 0:512], 1.0 / 64, EPS)
                for p_ in range(4):
                    for e_ in range(2):
                        stt(aqTz[64 * e_:64 * e_ + 64, p_, e_, :], qraw4[64 * e_:64 * e_ + 64, p_ * 128:(p_ + 1) * 128],
                            qg8[64 * e_:64 * e_ + 64, :], qsd4[64 * e_:64 * e_ + 64, p_ * 128:(p_ + 1) * 128], ALU.mult, ALU.mult)
                yield
                for hg in range(2):
                    pOa = psb[6]
                    for r_ in range(5):
                        st_tok = T0 - 512 + 128 * r_
                        sbk = st_tok // SBT
                        ttk = (st_tok % SBT) // 128
                        slk = sbk % NR
                        pSa = psb[5]
                        for hh in range(4):
                            h = 4 * hg + hh
                            p_, e_ = h // 2, h % 2
                            mm(pSa[:, hh * 128:(hh + 1) * 128], ident_bf[:], biasT[:, r_, h, :], start=True, stop=False)
                            mm(pSa[:, hh * 128:(hh + 1) * 128], akT[slk][:, p_, ttk * 128:(ttk + 1) * 128],
                               aqTz[:, p_, e_, :], start=False, stop=True)
                        pt_ = PTa[r_ % 2]
                        act(pt_[:], pSa[:, 0:512], AF.Exp, bias=(hmask if sbk < npre else 0.0))
                        for hh in range(4):
                            h = 4 * hg + hh
                            mm(pOa[:, hh * 65:hh * 65 + 65], pt_[:, hh * 128:(hh + 1) * 128], vaug[slk][:, ttk, h, :],
                               start=(r_ == 0 and hh == 0), stop=(r_ == 4), skip=True)
                        yield
                    ov = pOa[:, 0:260].rearrange("p (h c) -> p h c", c=65)
                    recip(rec[:, 12:16], ov[:, :, 64])
                    tten("dve", oatt[:, hg * 256:(hg + 1) * 256].rearrange("p (h d) -> p h d", d=64), ov[:, :, 0:64],
                         rec[:, 12:16].unsqueeze(2).to_broadcast([128, 4, 64]), ALU.mult)
                    yield
                if "oatt" in debug:
                    dmp = dbg_out.get("oatt")
                    if dmp is None:
                        dmp = nc.dram_tensor("dbg_oatt", [nown * SBT, 512], F32, kind="ExternalOutput").ap()
                        dbg_out["oatt"] = dmp
                    dma(dmp[row0:row0 + 128, :], oatt[:], wk=("dbgoatt", row0))

            def S4fin(blk):
                sb, tt = divmod(blk, TT)
                ob = sb - npre
                row0 = ob * SBT + tt * 128
                T0 = sb * SBT + tt * 128
                ns_ = nss[2]
                act(cat[:, 0:512], oatt[:], AF.Square, accum=ns_[:, 0:1])
                rstd_(ns_[:, 2:3], ns_[:, 0:1], 1.0 / 512, EPS)
                ts("dve", cat[:, 0:512], oatt[:], ns_[:, 2:3], None, ALU.mult)
                for kt in range(8):
                    tr(psT[:, kt * 128:(kt + 1) * 128], cat[:, kt * 128:(kt + 1) * 128], ident_bf[:])
                cp("act", catT[:], psT[:, 0:1024])
                dma(xres_[:], xs[T0:T0 + 128, :])
                yield
                for half in range(2):
                    pp = psb[5 + half]
                    for kt in range(8):
                        mm(pp[:, 0:512], catT[:, kt * 128:(kt + 1) * 128], Wout[:, kt, half * 512:(half + 1) * 512],
                           start=(kt == 0), stop=(kt == 7))
                    tten("dve", xres_[:, half * 512:(half + 1) * 512], pp[:, 0:512], xres_[:, half * 512:(half + 1) * 512], ALU.add)
                dma(out_d[row0:row0 + 128, :], xres_[:], wk=("dout", row0))
                yield

            def S34(blk):
                sb = blk // TT
                if sb < npre:
                    yield from S3a(blk)
                    return
                if os.environ.get("KSEQ"):
                    yield from S3a(blk)
                    yield from S4att(blk)
                    yield from S4fin(blk)
                    return
                g_att = S4att(blk)
                g_s3 = S3a(blk)
                km = os.environ.get("KMODE", "2")
                if km == "1":
                    next(g_att)
                    yield
                    next(g_att)
                    yield
                elif km == "2":
                    for _ in range(2):
                        next(g_att)
                        try:
                            next(g_s3)
                        except StopIteration:
                            pass
                        yield
                    for _ in g_s3:
                        yield
                subs = [[0.0, 0, g_s3], [0.0, 1, g_att]]
                while subs:
                    subs.sort(key=lambda t_: (t_[0], t_[1]))
                    ent = subs[0]
                    h0 = P.hop_fin
                    P.hop_fin = 0.0
                    try:
                        next(ent[2])
                    except StopIteration:
                        subs.pop(0)
                        P.hop_fin = max(h0, P.hop_fin)
                        continue
                    if P.hop_fin > 0.0:
                        ent[0] = P.hop_fin
                    P.hop_fin = max(h0, P.hop_fin)
                    yield
                yield from S4fin(blk)

            def run_round(gens):
                now = min(P.efree.values())
                live = [[now, i, g] for i, (g, w) in enumerate(gens)]
                cnt = len(live)
                while live:
                    live.sort(key=lambda t: (t[0], t[1]))
                    ent = live[0]
                    P.hop_fin = 0.0
                    try:
                        y = next(ent[2])
                    except StopIteration:
                        live.pop(0)
                        continue
                    if P.hop_fin > 0.0:
                        ent[0] = P.hop_fin
                    if isinstance(y, tuple) and y[0] == "spawn":
                        for g2 in y[1]:
                            live.append([ent[0], cnt, g2])
                            cnt += 1

            def S34pair(R):
                for t_ in range(TT):
                    yield from S34(R * TT + t_)

            run_round([(S1(sb_lo), 1)])
            if deep:
                for R in range(sb_lo, sb_hi + 1):
                    gens = []
                    if R - 1 >= sb_lo:
                        gens.append((S34pair(R - 1), 1))
                    if R < sb_hi:
                        gens += [(S2(R * TT + t_), 1) for t_ in range(TT)]
                    if R + 1 < sb_hi:
                        gens.append((S1(R + 1), 1))
                    run_round(gens)
            else:
                for r_ in range(sb_lo * TT, sb_hi * TT + 1):
                    gens = []
                    if r_ >= sb_lo * TT + 1:
                        gens.append((S34(r_ - 1), 1))
                    if r_ < sb_hi * TT:
                        gens.append((S2(r_), 1))
                    if r_ % TT == TT - 1 and (r_ + 1) // TT < sb_hi:
                        gens.append((S1((r_ + 1) // TT), 1))
                    run_round(gens)
            if os.environ.get("KDEBUG_SBUF"):
                print("MODEL phase1 end us", {k: round(v, 1) for k, v in P.efree.items()}, flush=True)
            barrier()

        if "nomlp" not in debug:
            with ExitStack() as s2:
                W1 = SB(s2, "W1", [128, 8, 4 * D], BF16)
                W2 = SB(s2, "W2", [128, 32, D], BF16)
                stg2 = [SB(s2, f"fstg{i}", [128, 2048]) for i in range(2)]
                hh2 = [[SB(s2, f"hh{c}_{i}", [128, D]) for i in range(TT)] for c in range(2)]
                nb = [SB(s2, f"nb{i}", [128, D], BF16) for i in range(TT)]
                nT2 = [SB(s2, f"nT{c}", [128, 8, SBT], BF16) for c in range(2)]
                aT = SB(s2, "aT", [128, 32, SBT], BF16)
                rz = [SB(s2, f"rz{i}", [128, SBT]) for i in range(2)]
                obuf = [SB(s2, f"obuf{i}", [128, D]) for i in range(2)]
                junk2 = SB(s2, "junk2", [128, D], BF16)
                fss = [SB(s2, f"fss{i}", [128, 4]) for i in range(2)]
                engs = ["dve", "pool"]
                ci = 0
                for kt in range(8):
                    for hf in range(2):
                        st_ = stg2[ci % 2]
                        dma(st_[:], w_ff1[kt * 128:(kt + 1) * 128, hf * 2048:(hf + 1) * 2048])
                        scale_cast(ci, W1[:, kt, hf * 2048:(hf + 1) * 2048], st_[:], fg[:, kt:kt + 1])
                        ci += 1
                pcnt2 = [0]

                def npz():
                    pcnt2[0] += 1
                    return psb[(0, 1, 2, 3, 6)[pcnt2[0] % 5]]
                def prologue(ob):
                    hh_, nT = hh2[ob % 2], nT2[ob % 2]
                    for tt in range(TT):
                        row0 = ob * SBT + tt * 128
                        fs = fss[tt % 2]
                        dma(hh_[tt][:], out_d[row0:row0 + 128, :], rk=("dout", row0))
                        act(junk2[:], hh_[tt][:], AF.Square, accum=fs[:, 0:1])
                        rstd_(fs[:, 2:3], fs[:, 0:1], 1.0 / D, EPS)
                        scale_cast(tt, nb[tt][:], hh_[tt][:], fs[:, 2:3])
                    for kt in range(8):
                        for tt in range(TT):
                            tr(psT[:, tt * 128:(tt + 1) * 128], nb[tt][:, kt * 128:(kt + 1) * 128], ident_bf[:])
                        cp("act" if kt % 2 else "dve", nT[:, kt, :], psT[:, 0:SBT])

                prologue(0)
                for f2 in range(16):
                    st_ = stg2[ci % 2]
                    dma(st_[:].rearrange("p (a n) -> p a n", a=2), w_ff2[f2 * 256:(f2 + 1) * 256, :].rearrange("(a p) n -> p a n", p=128))
                    cp(["dve", "act"][ci % 2], W2[:, 2 * f2:2 * f2 + 2, :], st_[:].rearrange("p (a n) -> p a n", a=2))
                    ci += 1
                for ob in range(nown):
                    hh_, nT = hh2[ob % 2], nT2[ob % 2]
                    for ft in range(32):
                        pz = npz()
                        for kt in range(8):
                            mm(pz[:, 0:SBT], W1[:, kt, ft * 128:(ft + 1) * 128], nT[:, kt, :], start=(kt == 0), stop=(kt == 7))
                        rz_ = rz[ft % 2]
                        act(rz_[:], pz[:, 0:SBT], AF.Relu)
                        tten("pool", aT[:, ft, :], rz_[:], rz_[:], ALU.mult)
                    if ob + 1 < nown:
                        prologue(ob + 1)
                    for tt in range(TT):
                        row0 = ob * SBT + tt * 128
                        ob_ = obuf[tt % 2]
                        for half in range(2):
                            po = psb[4 + half]
                            for ft in range(32):
                                mm(po[:, 0:512], aT[:, ft, tt * 128:(tt + 1) * 128], W2[:, ft, half * 512:(half + 1) * 512],
                                   start=(ft == 0), stop=(ft == 31))
                            tten("dve", ob_[:, half * 512:(half + 1) * 512], po[:, 0:512], hh_[tt][:, half * 512:(half + 1) * 512], ALU.add)
                        dma(out_d[row0:row0 + 128, :], ob_[:], wk=("dout", row0))
        P.finish()
        P.emit(nc)
    return nc, dbg_out


def _consts():
    i = np.arange(128)
    ident = np.eye(128, dtype=np.float32)
    ones = np.ones((128, 128), np.float32)
    tri = (i[:, None] <= i[None, :]).astype(np.float32)
    blk = ((i[:, None] // 64) == (i[None, :] // 64)).astype(np.float32)
    maskP = np.where(i[None, :] < i[:, None], 0.0, BIG).astype(np.float32)
    maskN = np.where(i[None, :] >= i[:, None], 0.0, -BIG).astype(np.float32)
    return np.ascontiguousarray(np.stack([ident, ones, tri, blk, maskP, maskN], axis=1))


def _bias_table(rel_bias):
    ext = np.concatenate([rel_bias, np.full((8, 1), -BIG, np.float32)], axis=1)
    p = np.arange(128)[:, None, None]
    r = np.arange(5)[None, :, None]
    q = np.arange(128)[None, None, :]
    k = 128 * r + p
    rel = np.clip(512 + q - k, -256, 256) + 256
    kc, qc = k // 64, q // 64
    valid = (kc >= qc) & (kc <= qc + 8)
    idx = np.where(valid, rel, 513)
    tab = ext[:, idx]
    tab = np.transpose(tab, (1, 2, 0, 3))
    return np.ascontiguousarray(tab.reshape(128, 5 * 8 * 128)).astype(np.float32)


def _small(inp, j):
    s = np.zeros((128, 96), np.float32)
    s[:, 0:8] = inp["mix_norm_gain"][0].reshape(8, 128).T
    s[:, 8:12] = inp["att_out_gain"][0].reshape(4, 128).T
    s[:, 12:16] = np.repeat(inp["dn_out_gain"][0][:, None], 4, axis=1)
    s[:, 16:24] = inp["ffn_norm_gain"][0].reshape(8, 128).T
    cwt = inp["dn_conv_w"][0]
    s[:, 24:72] = np.transpose(cwt.reshape(4, 12, 128), (2, 1, 0)).reshape(128, 48)
    s[:, 72] = np.tile(inp["att_q_gain"][0], 2)
    s[:, 73] = np.tile(inp["att_k_gain"][0], 2)
    s[:, 74:78] = np.broadcast_to(inp["dn_dt_bias"][0][None, :], (128, 4))
    s[:, 78:82] = np.broadcast_to(inp["dn_a_log"][0][None, :], (128, 4))
    if j == 0:
        s[:, 82] = np.float32(-BIG)
    return s


_CACHE = {}


def kernel(**inputs):
    inp = {k: np.asarray(v, dtype=np.float32) for k, v in inputs.items()}
    x = inp["x"]
    B, L, _ = x.shape
    nq = 4
    own_tok = L // nq
    nown = own_tok // SBT
    npre = (L - own_tok) // SBT
    key = (npre, nown)
    if key not in _CACHE:
        _CACHE[key] = build_program(npre, nown)[0]
    nc = _CACHE[key]
    cst = _consts()
    bias = _bias_table(inp["rel_bias"][0])
    in_maps = []
    for c in range(8):
        b, j = c // nq, c % nq
        xsn = np.zeros(((npre + nown) * SBT, D), np.float32)
        n_real = own_tok * (j + 1)
        xsn[xsn.shape[0] - n_real:, :] = x[b, :n_real, :]
        in_maps.append({
            "xs": xsn, "w_in": inp["w_in"][0], "w_out": inp["w_out"][0], "w_ff1": inp["w_ff1"][0],
            "w_ff2": inp["w_ff2"][0], "cst": cst, "biasT": bias, "small": _small(inp, j),
        })
    res = run_bass_kernel_spmd(nc, in_maps, core_ids=list(range(8)))
    out = np.zeros((B, L, D), np.float32)
    for c in range(8):
        b, j = c // nq, c % nq
        out[b, j * own_tok:(j + 1) * own_tok, :] = res.results[c]["out"]
    return out
```
